# Optimizing a Trainium2 kernel written in Bass

```python
import math
import jax
import jax.numpy as jnp
from jax import lax
import numpy as np

D_MODEL = 1024
BATCH = 16
SEQ = 2048
DEPTH = 2

CTX_LEN = 256
GRID_W = 64
EPS = 1e-6
NEG_INF = -1e30

GDN_HEADS = 4
GDN_DK = 128
GDN_DV = 128
GDN_QKV = GDN_HEADS * (2 * GDN_DK + GDN_DV)
GDN_WIDTH = GDN_HEADS * GDN_DV
SHORT_CONV = 5
CHUNK = 64

ATT_HEADS = 8
ATT_KV_HEADS = 2
ATT_GROUP = ATT_HEADS // ATT_KV_HEADS
ATT_HD = 64
ATT_WIDTH = ATT_HEADS * ATT_HD
ATT_KV_WIDTH = ATT_KV_HEADS * ATT_HD
WINDOW = 128
ATT_BLOCK = 128
ROPE_BASE = 10000.0
AXIS_DIM = ATT_HD // 2

STATE_SIZES = (GDN_QKV, 2 * GDN_HEADS, 2 * GDN_HEADS, ATT_KV_WIDTH, ATT_KV_WIDTH)
REST_SIZES = (GDN_WIDTH, ATT_WIDTH, ATT_WIDTH, D_MODEL, D_MODEL)
N_STATE = GDN_QKV + 4 * GDN_HEADS + 2 * ATT_KV_WIDTH
N_IN = N_STATE + GDN_WIDTH + 2 * ATT_WIDTH + 2 * D_MODEL

kernel_name = 'hybrid_gdn_swa_prefix_dit'


def _split_cols(p, sizes):
    return jnp.split(p, [int(i) for i in np.cumsum(sizes)[:-1]], axis=-1)


def _rmsnorm(x, w):
    xf = x.astype(jnp.float32)
    y = xf * lax.rsqrt(jnp.mean(xf * xf, axis=-1, keepdims=True) + EPS)
    return (y * w.astype(jnp.float32)).astype(x.dtype)


def _l2norm(x):
    xf = x.astype(jnp.float32)
    return (xf * lax.rsqrt(jnp.sum(xf * xf, axis=-1, keepdims=True) + EPS)).astype(x.dtype)


def _short_conv(x, w):
    return lax.conv_general_dilated(
        x, w[:, None, :].astype(x.dtype), window_strides=(1,),
        padding=[(SHORT_CONV // 2, SHORT_CONV // 2)],
        dimension_numbers=('NWC', 'WIO', 'NWC'), feature_group_count=x.shape[-1])


def _axial_rope(x, pos_row, pos_col):
    inv = ROPE_BASE ** (-jnp.arange(0, AXIS_DIM, 2, dtype=jnp.float32) / AXIS_DIM)
    xf = x.astype(jnp.float32)

    def rot(xa, pos):
        ang = pos[:, None] * inv[None, :]
        cos, sin = jnp.cos(ang)[None, :, None, :], jnp.sin(ang)[None, :, None, :]
        x1, x2 = jnp.split(xa, 2, axis=-1)
        return jnp.concatenate([x1 * cos - x2 * sin, x2 * cos + x1 * sin], axis=-1)

    xr, xc = jnp.split(xf, 2, axis=-1)
    return jnp.concatenate([rot(xr, pos_row), rot(xc, pos_col)], axis=-1).astype(x.dtype)


def _delta_state_update(s, u_n, w_n, kd_n, gl_n):
    v_new = u_n - jnp.einsum('bhck,bhkv->bhcv', w_n, s)
    s_new = s * jnp.exp(gl_n)[..., None, None] + jnp.einsum('bhck,bhcv->bhkv', kd_n, v_new)
    return s_new, v_new


def _gdn_chunked(q, k, v, g, beta, s0, with_output):
    b, l, h, _ = q.shape
    dv = v.shape[-1]
    n = l // CHUNK

    def to_chunks(t):
        t = t.reshape((b, n, CHUNK, h) + t.shape[3:])
        return jnp.moveaxis(t, (1, 3), (0, 2))

    qc, kc, vc = to_chunks(q), to_chunks(k), to_chunks(v)
    gc = jnp.cumsum(to_chunks(g), axis=-1)
    bc = to_chunks(beta)
    incl = jnp.tril(jnp.ones((CHUNK, CHUNK), bool))
    strict = jnp.tril(jnp.ones((CHUNK, CHUNK), bool), -1)
    diff = gc[..., :, None] - gc[..., None, :]
    decay = jnp.where(incl, jnp.exp(jnp.where(incl, diff, 0.0)), 0.0)
    kb = kc * bc[..., None]
    a_mat = jnp.where(strict, jnp.einsum('nbhik,nbhjk->nbhij', kb, kc) * decay, 0.0)
    eye = jnp.broadcast_to(jnp.eye(CHUNK, dtype=a_mat.dtype), a_mat.shape)
    t_mat = lax.linalg.triangular_solve(a_mat, eye, left_side=True, lower=True, unit_diagonal=True)
    u = jnp.einsum('nbhij,nbhjv->nbhiv', t_mat, vc * bc[..., None])
    w = jnp.einsum('nbhij,nbhjk->nbhik', t_mat, kb * jnp.exp(gc)[..., None])
    g_last = gc[..., -1]
    k_dec = kc * jnp.exp(g_last[..., None] - gc)[..., None]

    if with_output:
        qk = jnp.where(incl, jnp.einsum('nbhik,nbhjk->nbhij', qc, kc) * decay, 0.0)
        q_dec = qc * jnp.exp(gc)[..., None]

        def step(s, xs):
            u_n, w_n, kd_n, gl_n, qk_n, qd_n = xs
            s_new, v_new = _delta_state_update(s, u_n, w_n, kd_n, gl_n)
            o = jnp.einsum('bhck,bhkv->bhcv', qd_n, s) + jnp.einsum('bhij,bhjv->bhiv', qk_n, v_new)
            return s_new, o

        s_fin, o = lax.scan(step, s0, (u, w, k_dec, g_last, qk, q_dec))
        o = jnp.moveaxis(o, (0, 2), (1, 3)).reshape(b, l, h, dv)
        return o, s_fin

    def step_state(s, xs):
        u_n, w_n, kd_n, gl_n = xs
        s_new, _ = _delta_state_update(s, u_n, w_n, kd_n, gl_n)
        return s_new, None

    s_fin, _ = lax.scan(step_state, s0, (u, w, k_dec, g_last))
    return None, s_fin


def _gdn_inputs(qkv_raw, beta_raw, decay_raw, conv_w, a_log, dt_bias):
    b, l, _ = qkv_raw.shape
    qkv = jax.nn.silu(_short_conv(qkv_raw, conv_w))
    q, k, v = jnp.split(qkv, [GDN_HEADS * GDN_DK, 2 * GDN_HEADS * GDN_DK], axis=-1)
    q = _l2norm(q.reshape(b, l, GDN_HEADS, GDN_DK)) * (GDN_DK ** -0.5)
    k = _l2norm(k.reshape(b, l, GDN_HEADS, GDN_DK))
    v = v.reshape(b, l, GDN_HEADS, GDN_DV)
    beta = jax.nn.sigmoid(beta_raw.astype(jnp.float32)).reshape(b, l, 2, GDN_HEADS)
    g = -jnp.exp(a_log.astype(jnp.float32)) * jax.nn.softplus(
        decay_raw.astype(jnp.float32).reshape(b, l, 2, GDN_HEADS) + dt_bias.astype(jnp.float32))
    return q, k, v, g, beta


def _gdn_bidir(q, k, v, g, beta, s0_f, s0_b, with_output):
    q, k, v = q.astype(jnp.float32), k.astype(jnp.float32), v.astype(jnp.float32)
    o_f, s_f = _gdn_chunked(q, k, v, g[:, :, 0], beta[:, :, 0], s0_f, with_output)
    rev = lambda t: jnp.flip(t, axis=1)
    o_b, s_b = _gdn_chunked(rev(q), rev(k), rev(v), rev(g[:, :, 1]), rev(beta[:, :, 1]), s0_b, with_output)
    o = o_f + rev(o_b) if with_output else None
    return o, s_f, s_b


def _gdn_output(o, z, gdn_norm_w):
    b, l = o.shape[:2]
    o = _rmsnorm(o, gdn_norm_w).reshape(b, l, GDN_WIDTH).astype(z.dtype)
    return o * jax.nn.silu(z)


def _latent_attention(q, k, v, kc, vc, sink):
    b, l = q.shape[:2]
    nb = l // ATT_BLOCK
    scale = ATT_HD ** -0.5
    qb = q.reshape(b, nb, ATT_BLOCK, ATT_KV_HEADS, ATT_GROUP, ATT_HD)

    def windows(t):
        tp = jnp.pad(t, ((0, 0), (ATT_BLOCK, ATT_BLOCK), (0, 0), (0, 0)))
        tp = tp.reshape(b, nb + 2, ATT_BLOCK, ATT_KV_HEADS, ATT_HD)
        return jnp.concatenate([tp[:, :-2], tp[:, 1:-1], tp[:, 2:]], axis=2)

    kw, vw = windows(k), windows(v)
    s_loc = jnp.einsum('bnqhgd,bnkhd->bhgnqk', qb, kw).astype(jnp.float32) * scale
    qi = jnp.arange(ATT_BLOCK)[:, None] + ATT_BLOCK
    kj = jnp.arange(3 * ATT_BLOCK)[None, :]
    kabs = jnp.arange(nb)[:, None, None] * ATT_BLOCK + kj[None] - ATT_BLOCK
    valid = (jnp.abs(qi - kj) <= WINDOW)[None] & (kabs >= 0) & (kabs < l)
    s_loc = jnp.where(valid, s_loc, NEG_INF)
    s_ctx = jnp.einsum('bnqhgd,bkhd->bhgnqk', qb, kc).astype(jnp.float32) * scale
    sink_l = jnp.broadcast_to(
        sink.astype(jnp.float32).reshape(ATT_KV_HEADS, ATT_GROUP)[None, :, :, None, None, None],
        s_loc.shape[:-1] + (1,))
    p = jax.nn.softmax(jnp.concatenate([s_loc, s_ctx, sink_l], axis=-1), axis=-1).astype(v.dtype)
    nloc = 3 * ATT_BLOCK
    nctx = kc.shape[1]
    o = (jnp.einsum('bhgnqk,bnkhd->bnqhgd', p[..., :nloc], vw)
         + jnp.einsum('bhgnqk,bkhd->bnqhgd', p[..., nloc:nloc + nctx], vc))
    return o.reshape(b, l, ATT_WIDTH)


def _context_attention(q, kc, vc, sink):
    b, lc = q.shape[:2]
    s = jnp.einsum('bqhgd,bkhd->bhgqk', q, kc).astype(jnp.float32) * (ATT_HD ** -0.5)
    sink_c = jnp.broadcast_to(
        sink.astype(jnp.float32).reshape(ATT_KV_HEADS, ATT_GROUP)[None, :, :, None, None],
        s.shape[:-1] + (1,))
    p = jax.nn.softmax(jnp.concatenate([s, sink_c], axis=-1), axis=-1)[..., :-1].astype(vc.dtype)
    o = jnp.einsum('bhgqk,bkhd->bqhgd', p, vc)
    return o.reshape(b, lc, ATT_WIDTH)


def _merge(ya, yb, ga, gb, w_proj_a, w_proj_b, w_out):
    y = jax.nn.sigmoid(ga) * (ya @ w_proj_a) + jax.nn.sigmoid(gb) * (yb @ w_proj_b)
    return y @ w_out


def _layer(x, ctx, c, c_ctx, pos_row, pos_col, norm_w, w_mod, b_mod, w_in, conv_w, a_log,
           dt_bias, gdn_norm_w, q_norm_w, k_norm_w, sink, w_proj_a, w_proj_b, w_out, update_ctx):
    b, l, _ = x.shape
    lc = ctx.shape[1]
    shift, scale, gate = jnp.split(jax.nn.silu(c) @ w_mod + b_mod, 3, axis=-1)
    shift_c, scale_c, gate_c = jnp.split(jax.nn.silu(c_ctx) @ w_mod + b_mod, 3, axis=-1)
    h = _rmsnorm(x, norm_w) * (1.0 + scale[:, None]) + shift[:, None]
    hc = _rmsnorm(ctx, norm_w) * (1.0 + scale_c) + shift_c
    w_state, w_rest = w_in[:, :N_STATE], w_in[:, N_STATE:]

    qkv_c, beta_c, decay_c, kb_c, vb_c = _split_cols(hc @ w_state, STATE_SIZES)
    qa_c, ka_c, va_c, g_c, bt_c = _gdn_inputs(qkv_c, beta_c, decay_c, conv_w, a_log, dt_bias)
    s0 = jnp.zeros((b, GDN_HEADS, GDN_DK, GDN_DV), jnp.float32)
    oa_c, s_f, s_b = _gdn_bidir(qa_c, ka_c, va_c, g_c, bt_c, s0, s0, update_ctx)
    kb_c = _rmsnorm(kb_c.reshape(b, lc, ATT_KV_HEADS, ATT_HD), k_norm_w)
    vb_c = vb_c.reshape(b, lc, ATT_KV_HEADS, ATT_HD)

    qkv, beta_r, decay_r, kb, vb = _split_cols(h @ w_state, STATE_SIZES)
    za, qb, zb, ga, gb = _split_cols(h @ w_rest, REST_SIZES)
    qa, ka, va, g, bt = _gdn_inputs(qkv, beta_r, decay_r, conv_w, a_log, dt_bias)
    oa, _, _ = _gdn_bidir(qa, ka, va, g, bt, s_f, s_b, True)
    ya = _gdn_output(oa, za, gdn_norm_w)
    qb = _axial_rope(_rmsnorm(qb.reshape(b, l, ATT_HEADS, ATT_HD), q_norm_w), pos_row, pos_col)
    kb = _axial_rope(_rmsnorm(kb.reshape(b, l, ATT_KV_HEADS, ATT_HD), k_norm_w), pos_row, pos_col)
    vb = vb.reshape(b, l, ATT_KV_HEADS, ATT_HD)
    ob = _latent_attention(qb.reshape(b, l, ATT_KV_HEADS, ATT_GROUP, ATT_HD), kb, vb, kb_c, vb_c, sink)
    yb = ob * jax.nn.silu(zb)
    x = x + gate[:, None] * _merge(ya, yb, ga, gb, w_proj_a, w_proj_b, w_out)

    if update_ctx:
        za_c, qb_c, zb_c, ga_c, gb_c = _split_cols(hc @ w_rest, REST_SIZES)
        ya_c = _gdn_output(oa_c, za_c, gdn_norm_w)
        qb_c = _rmsnorm(qb_c.reshape(b, lc, ATT_HEADS, ATT_HD), q_norm_w)
        ob_c = _context_attention(qb_c.reshape(b, lc, ATT_KV_HEADS, ATT_GROUP, ATT_HD), kb_c, vb_c, sink)
        yb_c = ob_c * jax.nn.silu(zb_c)
        ctx = ctx + gate_c * _merge(ya_c, yb_c, ga_c, gb_c, w_proj_a, w_proj_b, w_out)
    return x, ctx


def setup_inputs(seed: int = 0) -> dict:
    key = jax.random.key(seed)
    ks = jax.random.split(key, 18)
    f32 = jnp.float32

    def nrm(k, shape, s):
        return jax.random.normal(k, shape, f32) * s

    x = nrm(ks[0], (BATCH, SEQ, D_MODEL), 1.0)
    c = nrm(ks[1], (BATCH, D_MODEL), 1.0)
    ctx = nrm(ks[2], (BATCH, CTX_LEN, D_MODEL), 1.0)
    c_ctx = nrm(ks[3], (D_MODEL,), 1.0)
    norm_w = 1.0 + nrm(ks[4], (DEPTH, D_MODEL), 0.02)
    w_mod = nrm(ks[5], (DEPTH, D_MODEL, 3 * D_MODEL), 0.5 * D_MODEL ** -0.5)
    b_mod = nrm(ks[6], (DEPTH, 3 * D_MODEL), 0.01)
    w_in = nrm(ks[7], (DEPTH, D_MODEL, N_IN), D_MODEL ** -0.5)
    conv_w = nrm(ks[8], (DEPTH, SHORT_CONV, GDN_QKV), SHORT_CONV ** -0.5)
    a_log = jnp.log(jax.random.uniform(ks[9], (DEPTH, 2, GDN_HEADS), f32, 1.0, 16.0))
    dt = jnp.exp(jax.random.uniform(ks[10], (DEPTH, 2, GDN_HEADS), f32, math.log(1e-3), math.log(1e-1)))
    dt_bias = dt + jnp.log(-jnp.expm1(-dt))
    gdn_norm_w = 1.0 + nrm(ks[11], (DEPTH, GDN_DV), 0.02)
    q_norm_w = 1.0 + nrm(ks[12], (DEPTH, ATT_HD), 0.02)
    k_norm_w = 1.0 + nrm(ks[13], (DEPTH, ATT_HD), 0.02)
    sink = nrm(ks[14], (DEPTH, ATT_HEADS), 0.5)
    w_proj_a = nrm(ks[15], (DEPTH, GDN_WIDTH, D_MODEL), GDN_WIDTH ** -0.5)
    w_proj_b = nrm(ks[16], (DEPTH, ATT_WIDTH, D_MODEL), ATT_WIDTH ** -0.5)
    w_out = nrm(ks[17], (DEPTH, D_MODEL, D_MODEL), D_MODEL ** -0.5)
    return {'x': x, 'c': c, 'ctx': ctx, 'c_ctx': c_ctx, 'norm_w': norm_w, 'w_mod': w_mod,
            'b_mod': b_mod, 'w_in': w_in, 'conv_w': conv_w, 'a_log': a_log, 'dt_bias': dt_bias,
            'gdn_norm_w': gdn_norm_w, 'q_norm_w': q_norm_w, 'k_norm_w': k_norm_w, 'sink': sink,
            'w_proj_a': w_proj_a, 'w_proj_b': w_proj_b, 'w_out': w_out}


def reference(x, c, ctx, c_ctx, norm_w, w_mod, b_mod, w_in, conv_w, a_log, dt_bias, gdn_norm_w,
              q_norm_w, k_norm_w, sink, w_proj_a, w_proj_b, w_out):
    l = x.shape[1]
    rows = l // GRID_W
    pos_row = jnp.repeat(jnp.arange(rows, dtype=jnp.float32), GRID_W)
    pos_col = jnp.tile(jnp.arange(GRID_W, dtype=jnp.float32), rows)
    for i in range(DEPTH):
        x, ctx = _layer(x, ctx, c, c_ctx, pos_row, pos_col, norm_w[i], w_mod[i], b_mod[i], w_in[i],
                        conv_w[i], a_log[i], dt_bias[i], gdn_norm_w[i], q_norm_w[i], k_norm_w[i],
                        sink[i], w_proj_a[i], w_proj_b[i], w_out[i], update_ctx=(i < DEPTH - 1))
    return x
```

```python
import math
from contextlib import ExitStack
import numpy as np
import concourse.bass as bass
import concourse.mybir as mybir
from concourse.bass_utils import run_bass_kernel_spmd

F32 = mybir.dt.float32
F32R = mybir.dt.float32r
BF16 = mybir.dt.bfloat16
ALU = mybir.AluOpType
AF = mybir.ActivationFunctionType
AX = mybir.AxisListType

NCORES = 8
D = 1024
L = 2048
LC = 256
LT = L + LC
NT = LT // 128
NTC = LC // 128
DEPTH = 2
N_STATE = 1808
N_IN = 5392
EPS = 1e-6
BIG = 30000.0
GDN_PHASE = 1
O_QKV, O_BD, O_KVB, O_ZA, O_QB, O_ZB, O_GA, O_GB = 0, 1536, 1552, 1808, 2320, 2832, 3344, 4368
TP_BD, TP_KB, TP_VB, TP_ZA, TP_QB, TP_ZB, TP_W = 0, 16, 144, 272, 784, 1296, 1808


class Res:
    __slots__ = ("w", "r")

    def __init__(self):
        self.w = None
        self.r = []


class Prog:
    def __init__(self, nc):
        self.nc = nc
        self.eng = {"pe": nc.tensor, "act": nc.scalar, "dve": nc.vector, "pool": nc.gpsimd, "sp": nc.sync}
        self.sem = {k: nc.alloc_semaphore("s_" + k) for k in self.eng}
        self.cnt = {k: 0 for k in self.eng}
        self.waited = {}
        self.dma_sems = {}
        self.sem_cnt = {}
        self.free_sems = []
        self.keep = []
        self.pool_sems = []
        self.pool_done = 0
        self.ninst = 0

    def _wait(self, e, deps):
        best = {}
        for (s, v) in deps:
            k = id(s)
            if k not in best or best[k][1] < v:
                best[k] = (s, v)
        for k, (s, v) in best.items():
            if self.waited.get((e, k), 0) < v:
                self.eng[e].wait_ge(s, v)
                self.waited[(e, k)] = v

    @staticmethod
    def _deps(reads, writes):
        deps = []
        for r in reads:
            if r.w is not None:
                deps.append(r.w)
        for w in writes:
            if w.w is not None:
                deps.append(w.w)
            deps.extend(w.r)
        return deps

    @staticmethod
    def _mark(tok, reads, writes):
        for r in reads:
            r.r.append(tok)
            if len(r.r) > 24:
                best = {}
                for (s, v) in r.r:
                    if id(s) not in best or best[id(s)][1] < v:
                        best[id(s)] = (s, v)
                r.r = list(best.values())
        for w in writes:
            w.w = tok
            w.r = []

    def op(self, e, fn, reads=(), writes=()):
        self._wait(e, self._deps(reads, writes))
        inst = fn(self.eng[e])
        self.cnt[e] += 1
        self.ninst += 1
        inst.then_inc(self.sem[e], 1)
        tok = (self.sem[e], self.cnt[e])
        self._mark(tok, reads, writes)
        return tok

    def dma(self, e, out, in_, reads=(), writes=(), key=None):
        self._wait(e, self._deps(reads, writes))
        if e == "pool":
            sm = self.nc.alloc_semaphore("q%d" % len(self.pool_sems))
            self.pool_sems.append(sm)
            self.eng[e].dma_start(out=out, in_=in_).then_inc(sm, 16)
            tok = (sm, 16)
            self._mark(tok, reads, writes)
            return tok
        kres = key if key is not None else (writes[0] if writes else reads[0])
        k = id(kres)
        if k not in self.dma_sems:
            if self.free_sems:
                self.dma_sems[k] = self.free_sems.pop()
            else:
                sm = self.nc.alloc_semaphore("d%d" % len(self.sem_cnt))
                self.sem_cnt[id(sm)] = 0
                self.dma_sems[k] = sm
            self.keep.append(kres)
        s = self.dma_sems[k]
        self.sem_cnt[id(s)] += 16
        self.eng[e].dma_start(out=out, in_=in_).then_inc(s, 16)
        tok = (s, self.sem_cnt[id(s)])
        self._mark(tok, reads, writes)
        return tok

    def barrier(self):
        toks = [(self.sem[k], self.cnt[k]) for k in self.eng if self.cnt[k] > 0]
        toks += [(sm, self.sem_cnt[id(sm)]) for sm in self.dma_sems.values()]
        toks += [(sm, 16) for sm in self.pool_sems[self.pool_done:]]
        self.pool_done = len(self.pool_sems)
        for e in self.eng:
            self._wait(e, toks)
        self.free_sems.extend(self.dma_sems.values())
        self.dma_sems = {}
        self.keep = []


def build(NB=2, dbg=None):
    nc = bass.Bass("TRN2", target_bir_lowering=False)
    p = Prog(nc)
    R = NB + 1

    def din(name, shape, dt=F32):
        return nc.dram_tensor(name, list(shape), dt, kind="ExternalInput").ap()

    x_in = din("x", [NB, L, D])
    ctx_in = din("ctx", [NB, LC, D])
    c_in = din("c", [R, D])
    norm_w = din("norm_w", [DEPTH, D])
    w_mod = din("w_mod", [DEPTH, D, 3 * D])
    b_mod = din("b_mod", [DEPTH, 3 * D])
    w_in = din("w_in", [DEPTH, D, N_IN])
    conv_w = din("conv_w", [DEPTH, 5, 1536])
    a_log = din("a_log", [DEPTH, 8])
    dt_bias = din("dt_bias", [DEPTH, 8])
    gdn_norm_w = din("gdn_norm_w", [DEPTH, 128])
    q_norm_w = din("q_norm_w", [DEPTH, 64])
    k_norm_w = din("k_norm_w", [DEPTH, 64])
    sink = din("sink", [DEPTH, 8])
    w_proj_a = din("w_proj_a", [DEPTH, 512, D])
    w_proj_b = din("w_proj_b", [DEPTH, 512, D])
    w_out = din("w_out", [DEPTH, D, D])
    rope_in = din("rope", [128, 16, 2, 2, 16])
    gmask_in = din("gmask", [128, 2, 7, 2, 128])
    y_out = nc.dram_tensor("y", [NB, L, D], F32, kind="ExternalOutput").ap()

    def scratch(name, shape, dt=F32):
        return nc.dram_tensor(name, list(shape), dt, kind="Internal").ap()

    x1_s = scratch("x1_s", [NB, L, D])
    ctx1_s = scratch("ctx1_s", [NB, LC, D])
    mods_s = scratch("mods_s", [R, 3 * D])
    qkvT_s = scratch("qkvT_s", [1536, LT])
    gT_s = scratch("gT_s", [2048, LT], BF16)
    tokp_s = scratch("tokp_s", [LT, TP_W])
    ktok_s = scratch("ktok_s", [NT, 128, 4, 128], BF16)
    vtok_s = scratch("vtok_s", [NT, 128, 4, 128], BF16)
    o_s = scratch("o_s", [2, LT, 512])
    yb_s = scratch("yb_s", [LT, 512], BF16)
    r_yb_s = [Res() for _ in range(NT)]
    r_ktok_s = Res()
    r_vtok_s = Res()
    r_o_s = [[Res() for _ in range(NT)] for _ in range(2)]
    r_x1 = [Res() for _ in range(NB)]
    r_ctx1 = [Res() for _ in range(NB)]
    r_mods = Res()
    r_qkvT = Res()
    r_gT = Res()
    r_tokp = Res()
    dbg_out = {}
    if dbg:
        for name, shape in dbg.items():
            dbg_out[name] = nc.dram_tensor("dbg_" + name, list(shape), F32, kind="ExternalOutput").ap()

    top = ExitStack()

    uid = [0]

    def sb(stack, name, shape, dt=F32):
        uid[0] += 1
        return stack.enter_context(nc.sbuf_tensor("%s_%d" % (name, uid[0]), list(shape), dt))

    psum_all = nc.alloc_psum_tensor("psum_all", [128, 8, 512], F32)
    banks = [psum_all[:, i, :] for i in range(8)]
    bank_r = [Res() for _ in range(8)]
    bank_i = [0]
    nrot = [8]

    def ps():
        i = bank_i[0] % nrot[0]
        bank_i[0] += 1
        return banks[i], bank_r[i]

    def ps2():
        while (bank_i[0] % nrot[0]) % 2 != 0:
            bank_i[0] += 1
        i = bank_i[0] % nrot[0]
        bank_i[0] += 2
        return (banks[i], banks[i + 1]), (bank_r[i], bank_r[i + 1])

    free_banks = list(range(8))

    def gps():
        for i in free_banks:
            if (i ^ 1) not in free_banks:
                free_banks.remove(i)
                return banks[i], bank_r[i]
        i = free_banks.pop(0)
        return banks[i], bank_r[i]

    def gps2():
        for i in (0, 2, 4, 6):
            if i in free_banks and (i + 1) in free_banks:
                free_banks.remove(i)
                free_banks.remove(i + 1)
                return psum_all[:, i:i + 2, :], (bank_r[i], bank_r[i + 1]), i
        raise AssertionError("no aligned PSUM bank pair free")

    def grel2(i):
        assert i not in free_banks and (i + 1) not in free_banks
        free_banks.extend([i, i + 1])

    def grel(*bks):
        for bk_ in bks:
            for i in range(8):
                if banks[i] is bk_:
                    assert i not in free_banks
                    free_banks.append(i)

    def run_pipelined(gen_list, depth, stagger=0):
        pending = list(gen_list)
        live = []
        since = stagger
        while pending or live:
            if pending and pending[0] is None:
                if not live:
                    pending.pop(0)
                    since = stagger
            elif pending and len(live) < depth and since >= stagger:
                live.append(pending.pop(0))
                since = 0
            since += 1
            for g_ in list(live):
                try:
                    next(g_)
                except StopIteration:
                    live.remove(g_)

    ident_f = sb(top, "ident_f", [128, 128]); r_const = Res()
    ident_b = sb(top, "ident_b", [128, 128], BF16)
    ones_f = sb(top, "ones_f", [128, 128])
    ones_b = sb(top, "ones_b", [128, 128], BF16)
    m_incl = sb(top, "m_incl", [128, 2, 128])
    m_strict = sb(top, "m_strict", [128, 2, 128])
    tri_f = sb(top, "tri_f", [128, 2, 128])
    mk_prev = sb(top, "mk_prev", [128, 128], BF16)
    mk_next = sb(top, "mk_next", [128, 128], BF16)
    rope_t = sb(top, "rope_t", [128, 16, 2, 2, 16])
    gmask = sb(top, "gmask_sb", [128, 2, 7, 2, 128], BF16)
    scT = sb(top, "scT", [128, 8, R])

    def cst(fn):
        p.op("pool", fn, reads=[r_const], writes=[r_const])

    def asel(out_ap, op_, fill, pat, cm):
        cst(lambda g: g.affine_select(out=out_ap, in_=out_ap, compare_op=op_, fill=fill, base=0, pattern=pat, channel_multiplier=cm))
    cst(lambda g: g.memset(ident_f[:], 0.0))
    asel(ident_f[:], ALU.not_equal, 1.0, [[-1, 128]], 1)
    cst(lambda g: g.memset(ones_f[:], 1.0))
    cst(lambda g: g.memset(ones_b[:], 1.0))
    cst(lambda g: g.memset(m_incl[:], 0.0))
    asel(m_incl[:, 0, :], ALU.is_ge, BIG, [[-1, 128]], 1)
    asel(m_incl[:, 1, :], ALU.is_ge, BIG, [[1, 128]], -1)
    cst(lambda g: g.memset(m_strict[:], -1.0))
    asel(m_strict[:, 0, :], ALU.is_gt, 0.0, [[-1, 128]], 1)
    asel(m_strict[:, 1, :], ALU.is_gt, 0.0, [[1, 128]], -1)
    cst(lambda g: g.memset(tri_f[:], 1.0))
    asel(tri_f[:, 0, :], ALU.is_ge, 0.0, [[1, 128]], -1)
    asel(tri_f[:, 1, :], ALU.is_ge, 0.0, [[-1, 128]], 1)
    cst(lambda g: g.memset(mk_prev[:], 1.0))
    asel(mk_prev[:], ALU.is_ge, 0.0, [[-1, 128]], 1)
    cst(lambda g: g.memset(mk_next[:], 1.0))
    asel(mk_next[:], ALU.is_ge, 0.0, [[1, 128]], -1)
    p.op("dve", lambda v: v.tensor_copy(ident_b[:], ident_f[:]), reads=[r_const], writes=[r_const])
    p.dma("sp", rope_t[:], rope_in, writes=[r_const])
    p.dma("pool", gmask[:], gmask_in, writes=[r_const])

    with ExitStack() as st:
        crow = sb(st, "crow", [R, D]); r_crow = Res()
        p.dma("sp", crow[:], c_in, writes=[r_crow])
        p.op("act", lambda a: a.activation(out=crow[:], in_=crow[:], func=AF.Silu), reads=[r_crow], writes=[r_crow])
        bk, br = ps()

        def tr(t):
            for k in range(8):
                i = t.transpose(bk[:, k * R:(k + 1) * R], crow[0:R, k * 128:(k + 1) * 128], ident_f[0:R, 0:R])
            return i
        p.op("pe", tr, reads=[r_crow, r_const], writes=[br])
        p.op("dve", lambda v: v.tensor_copy(scT[:].rearrange("p k r -> p (k r)"), bk[:, 0:8 * R]), reads=[br], writes=[r_const])
        p.barrier()


    def layer_mods(l):
        with ExitStack() as st:
            wm = [sb(st, "wm%d" % i, [128, 8, 512]) for i in range(2)]
            r_wm = [Res(), Res()]
            bm = sb(st, "bm", [R, 3 * D]); r_bm = Res()
            rows = sb(st, "rows", [R, 3 * D]); r_rows = Res()
            p.dma("sp", bm[:], b_mod[l:l + 1, :].to_broadcast([R, 3 * D]), writes=[r_bm])

            def ld(j):
                p.dma("sp", wm[j % 2][:], w_mod[l, :, j * 512:(j + 1) * 512].rearrange("(k p) n -> p k n", p=128),
                      writes=[r_wm[j % 2]])
            ld(0)
            for j in range(6):
                if j + 1 < 6:
                    ld(j + 1)
                bk, br = ps()

                def mm(t, j=j, bk=bk):
                    for k in range(8):
                        i = t.matmul(bk[0:R, :], scT[:, k, :], wm[j % 2][:, k, :], start=(k == 0), stop=(k == 7))
                    return i
                p.op("pe", mm, reads=[r_wm[j % 2], r_const], writes=[br])
                p.op("dve", lambda v, j=j, bk=bk: v.tensor_tensor(out=rows[:, j * 512:(j + 1) * 512], in0=bk[0:R, :],
                                                                   in1=bm[:, j * 512:(j + 1) * 512], op=ALU.add),
                     reads=[br, r_bm], writes=[r_rows])
            p.dma("sp", mods_s, rows[:], reads=[r_rows], writes=[r_mods])
            p.barrier()

    def block(b, l):
        x_src = x_in[b] if l == 0 else x1_s[b]
        c_src = ctx_in[b] if l == 0 else ctx1_s[b]
        x_dst = x1_s[b] if l == 0 else y_out[b]
        src_res = [] if l == 0 else [r_x1[b], r_ctx1[b]]
        dst_res = [r_x1[b]] if l == 0 else []
        upd_ctx = (l == 0)
        t_lo = 0 if upd_ctx else NTC

        def tok_src(t):
            return c_src[t * 128:(t + 1) * 128, :] if t < NTC else x_src[(t - NTC) * 128:(t - NTC + 1) * 128, :]

        with ExitStack() as st:
            hT = sb(st, "hT", [128, 8, LT], BF16)
            r_hT = [Res() for _ in range(NT)]
            wt = [sb(st, "wt%d" % i, [128, 8, 512], BF16) for i in range(2)]
            r_wt = [Res(), Res()]
            wtiles = [(0, 512, "qkv", 0), (512, 512, "qkv", 512), (1024, 512, "qkv", 1024),
                      (O_BD, 272, "tm", TP_BD), (O_ZA, 512, "tmz", TP_ZA), (O_QB, 512, "tm", TP_QB),
                      (O_ZB, 512, "tmz", TP_ZB)]
            wtiles += [(O_GA + i * 512, 512, "g", i * 512) for i in range(4)]

            def ldw(j):
                c0, w, _, _ = wtiles[j]
                p.dma("pool", wt[j % 2][:, :, 0:w], w_in[l, :, c0:c0 + w].rearrange("(k p) n -> p k n", p=128),
                      writes=[r_wt[j % 2]])
            ldw(0)
            with ExitStack() as st1:
                A_bc = [sb(st1, "A_bc%d" % i, [128, D]) for i in range(2)]
                S_bc = [sb(st1, "S_bc%d" % i, [128, D]) for i in range(2)]
                r_bc = Res()
                nw = sb(st1, "nw_bc", [128, D]); r_nw = Res()
                tmp_l = [sb(st1, "sc_bc%d" % i, [128, D]) for i in range(2)]; r_tmp_l = [Res(), Res()]
                p.dma("act", nw[:], norm_w[l:l + 1, :].to_broadcast([128, D]), writes=[r_nw])
                for i, r in enumerate((b, NB)):
                    p.dma("act", S_bc[i][:], mods_s[r:r + 1, 0:D].to_broadcast([128, D]), reads=[r_mods], writes=[r_bc], key=r_tmp_l[i])
                    p.dma("act", tmp_l[i][:], mods_s[r:r + 1, D:2 * D].to_broadcast([128, D]), reads=[r_mods], writes=[r_tmp_l[i]])
                    p.op("dve", lambda v, i=i: v.scalar_tensor_tensor(out=A_bc[i][:], in0=tmp_l[i][:], scalar=1.0, in1=nw[:],
                                                                       op0=ALU.add, op1=ALU.mult),
                         reads=[r_tmp_l[i], r_nw], writes=[r_bc])
                NXB = 4
                xt = [sb(st1, "xt%d" % i, [128, D]) for i in range(NXB)]
                r_xt = [Res() for _ in range(NXB)]
                r_xt2 = [Res() for _ in range(NXB)]
                junk_l = [sb(st1, "junk%d" % i, [128, D]) for i in range(3)]; r_junk_l = [Res() for _ in range(3)]
                t32_l = [sb(st1, "t32%d" % i, [128, D]) for i in range(3)]; r_t32_l = [Res() for _ in range(3)]
                hb_l = [sb(st1, "hb%d" % i, [128, D], BF16) for i in range(3)]; r_hb_l = [Res() for _ in range(3)]
                ss_l = [sb(st1, "ss%d" % i, [128, 2]) for i in range(3)]; r_ss_l = [Res() for _ in range(3)]

                D1 = 3

                def p1_tile(t):
                    i = 1 if t < NTC else 0
                    X, rX_ = xt[t % NXB], r_xt[t % NXB]
                    j = t % D1
                    junk, r_junk, t32, r_t32 = junk_l[j], r_junk_l[j], t32_l[j], r_t32_l[j]
                    hb, r_hb, ss, r_ss = hb_l[j], r_hb_l[j], ss_l[j], r_ss_l[j]
                    p.dma("sp", X[0:64, :], tok_src(t)[0:64, :], reads=src_res, writes=[rX_])
                    p.dma("act", X[64:128, :], tok_src(t)[64:128, :], reads=src_res, writes=[rX_])
                    yield
                    p.op("act", lambda a: a.activation(out=junk[:], in_=X[:], func=AF.Square, accum_out=ss[:, 0:1]),
                         reads=[rX_], writes=[r_junk, r_ss])
                    yield
                    p.op("dve", lambda v: v.tensor_scalar(out=ss[:, 1:2], in0=ss[:, 0:1], scalar1=1.0 / D, scalar2=EPS,
                                                          op0=ALU.mult, op1=ALU.add), reads=[r_ss], writes=[r_ss])
                    yield
                    p.op("act", lambda a: a.activation(out=ss[:, 1:2], in_=ss[:, 1:2], func=AF.Sqrt), reads=[r_ss], writes=[r_ss])
                    yield
                    p.op("dve", lambda v: v.reciprocal(ss[:, 1:2], ss[:, 1:2]), reads=[r_ss], writes=[r_ss])
                    p.op("dve", lambda v: v.scalar_tensor_tensor(out=t32[:], in0=X[:], scalar=ss[:, 1:2], in1=A_bc[i][:],
                                                                 op0=ALU.mult, op1=ALU.mult),
                         reads=[rX_, r_ss, r_bc], writes=[r_t32])
                    yield
                    p.op("pool", lambda g: g.tensor_tensor(out=hb[:], in0=t32[:], in1=S_bc[i][:], op=ALU.add),
                         reads=[r_t32, r_bc], writes=[r_hb])
                    yield
                    bk, br = gps()
                    bkb = bk[:].bitcast(BF16)

                    def tr(tt):
                        for k in range(8):
                            ii = tt.transpose(bkb[:, k * 128:(k + 1) * 128], hb[:, k * 128:(k + 1) * 128], ident_b[:])
                        return ii
                    p.op("pe", tr, reads=[r_hb, r_const], writes=[br])
                    yield
                    p.op("act", lambda a: a.copy(hT[:, :, t * 128:(t + 1) * 128], bkb.rearrange("p (k n) -> p k n", k=8)),
                         reads=[br], writes=[r_hT[t]])
                    grel(bk)
                    yield
                run_pipelined([p1_tile(t) for t in range(NT)], D1, stagger=2)
                p.barrier()
            if dbg and "hT" in dbg and (b, l) == dbg_bl:
                with ExitStack() as sd:
                    d32 = sb(sd, "d32", [128, 8, LT]); r_d = Res()
                    p.op("dve", lambda v: v.tensor_copy(d32[:], hT[:]), reads=r_hT, writes=[r_d])
                    tk = p.dma("sp", dbg_out["hT"], d32[:], reads=[r_d])
                    p.barrier()

            with ExitStack() as st2:
                stg = [sb(st2, "stg%d" % i, [128, LT]) for i in range(2)]
                r_stg = [Res(), Res()]
                stgb = [sb(st2, "stgb%d" % i, [128, LT], BF16) for i in range(2)]
                r_stgb = [Res(), Res()]
                stm = [sb(st2, "stm%d" % i, [128, 512]) for i in range(2)]
                r_stm = [Res(), Res()]
                nfm = 0
                ntm = 0
                for j, (c0, w, kind, doff) in enumerate(wtiles):
                    if j + 1 < len(wtiles):
                        ldw(j + 1)
                    W = wt[j % 2]
                    rW = r_wt[j % 2]
                    if kind in ("qkv", "g"):
                        tok0 = 0 if (kind == "qkv" or upd_ctx) else LC
                        groups = []
                        a0 = tok0
                        while a0 < LT:
                            groups.append((a0, min(512, LT - a0)))
                            a0 += 512
                        for c in range(w // 128):
                            if kind == "qkv":
                                S_, rS = stg[nfm % 2], r_stg[nfm % 2]
                            else:
                                S_, rS = stgb[nfm % 2], r_stgb[nfm % 2]
                            nfm += 1
                            for (g0, gw) in groups:
                                bk, br = ps()

                                def mm(t, bk=bk, W=W, c=c, g0=g0, gw=gw):
                                    for k in range(8):
                                        i = t.matmul(bk[:, 0:gw], W[:, k, c * 128:(c + 1) * 128], hT[:, k, g0:g0 + gw],
                                                     start=(k == 0), stop=(k == 7))
                                    return i
                                p.op("pe", mm, reads=[rW] + r_hT[g0 // 128:(g0 + gw) // 128], writes=[br])
                                if kind == "qkv":
                                    p.op("dve", lambda v, bk=bk, S_=S_, g0=g0, gw=gw: v.tensor_copy(S_[:, g0:g0 + gw], bk[:, 0:gw]),
                                         reads=[br], writes=[rS])
                                else:
                                    p.op("act", lambda a, bk=bk, S_=S_, g0=g0, gw=gw: a.activation(out=S_[:, g0:g0 + gw], in_=bk[:, 0:gw],
                                                                                                   func=AF.Sigmoid),
                                         reads=[br], writes=[rS])
                            row0 = doff + c * 128
                            if kind == "qkv":
                                p.dma("sp", qkvT_s[row0:row0 + 128, :], S_[:], reads=[rS], writes=[r_qkvT])
                            else:
                                p.dma("sp", gT_s[row0:row0 + 128, tok0:LT], S_[:, tok0:LT], reads=[rS], writes=[r_gT])
                    else:
                        t0 = 0 if (kind == "tm" and c0 == O_BD) else t_lo
                        for t in range(t0, NT):
                            bk, br = ps()
                            S_, rS = stm[ntm % 2], r_stm[ntm % 2]
                            ntm += 1

                            def mm(tt, bk=bk, W=W, t=t, w=w):
                                for k in range(8):
                                    i = tt.matmul(bk[:, 0:w], hT[:, k, t * 128:(t + 1) * 128], W[:, k, 0:w],
                                                  start=(k == 0), stop=(k == 7))
                                return i
                            p.op("pe", mm, reads=[rW, r_hT[t]], writes=[br])
                            if kind == "tmz":
                                p.op("act", lambda a, bk=bk, S_=S_, w=w: a.activation(out=S_[:, 0:w], in_=bk[:, 0:w], func=AF.Silu),
                                     reads=[br], writes=[rS])
                            else:
                                p.op("dve", lambda v, bk=bk, S_=S_, w=w: v.tensor_copy(S_[:, 0:w], bk[:, 0:w]), reads=[br], writes=[rS])
                            p.dma("sp", tokp_s[t * 128:(t + 1) * 128, doff:doff + w], S_[:, 0:w], reads=[rS], writes=[r_tokp])
                p.barrier()
        if dbg and (b, l) == dbg_bl:
            for nm, src, rr in (("qkvT", qkvT_s, r_qkvT), ("tokp", tokp_s, r_tokp)):
                if nm in dbg:
                    p.dma("sp", dbg_out[nm], src, reads=[rr])
            p.barrier()

        def dbg_dump(name, tile_ap, reads, shape, direct=False):
            if dbg and name in dbg and (b, l) == dbg_bl and direct:
                p.dma("sp", dbg_out[name], tile_ap, reads=reads)
                p.barrier()
            elif dbg and name in dbg and (b, l) == dbg_bl:
                with ExitStack() as sd:
                    d32 = sb(sd, "d32_" + name, shape); r_d = Res()
                    p.op("dve", lambda v: v.tensor_copy(d32[:], tile_ap), reads=reads, writes=[r_d])
                    p.dma("sp", dbg_out[name], d32[:], reads=[r_d])
                    p.barrier()

        stYb = ExitStack()
        ybT = sb(stYb, "ybT", [128, 4, LT], BF16); r_ybT = [Res() for _ in range(NT)]

        with ExitStack() as stG:
            qT = sb(stG, "qT", [128, 4, LT], BF16)
            kT = sb(stG, "kT", [128, 4, LT], BF16)
            r_qT = [Res() for _ in range(4)]
            r_kT = [Res() for _ in range(4)]
            beta = sb(stG, "beta", [128, NT, 8])
            gcs = sb(stG, "gcs", [128, NT, 8])
            egc = sb(stG, "egc", [128, NT, 8])
            bge = sb(stG, "bge", [128, NT, 8])
            ekd = sb(stG, "ekd", [128, NT, 8])
            egl = sb(stG, "egl", [128, NT, 8])
            r_gate = Res()
            sGt = ExitStack()
            bdt = sb(sGt, "bdt", [128, NT, 16]); r_bdt = Res()
            ab = sb(sGt, "ab", [128, 16]); r_ab = Res()
            gg = sb(sGt, "gg", [128, NT, 8]); r_gg = Res()
            gg2 = sb(sGt, "gg2", [128, 2, NT, 4])
            gl = sb(sGt, "gl", [128, NT, 8]); r_gl = Res()

            def gates_gen():
                p.dma("sp", bdt[:], tokp_s[:, 0:16].rearrange("(t p) c -> p t c", p=128), reads=[r_tokp], writes=[r_bdt])
                p.dma("sp", ab[:, 0:8], a_log[l:l + 1, :].to_broadcast([128, 8]), writes=[r_ab])
                p.dma("sp", ab[:, 8:16], dt_bias[l:l + 1, :].to_broadcast([128, 8]), writes=[r_ab])
                yield
                p.op("act", lambda a: a.activation(out=beta[:], in_=bdt[:, :, 0:8], func=AF.Sigmoid), reads=[r_bdt], writes=[r_gate])
                p.op("dve", lambda v: v.tensor_tensor(out=gg[:], in0=bdt[:, :, 8:16], in1=ab[:, 8:16].unsqueeze(1).to_broadcast([128, NT, 8]),
                                                      op=ALU.add), reads=[r_bdt, r_ab], writes=[r_gg])
                yield
                p.op("act", lambda a: a.activation(out=gg[:], in_=gg[:], func=AF.Exp), reads=[r_gg], writes=[r_gg])
                yield
                p.op("dve", lambda v: v.tensor_scalar_add(gg[:], gg[:], 1.0), reads=[r_gg], writes=[r_gg])
                yield
                p.op("act", lambda a: a.activation(out=gg[:], in_=gg[:], func=AF.Ln), reads=[r_gg], writes=[r_gg])
                p.op("act", lambda a: a.activation(out=ab[:, 0:8], in_=ab[:, 0:8], func=AF.Exp), reads=[r_ab], writes=[r_ab])
                yield
                for d_ in range(2):
                    p.op("dve", lambda v, d_=d_: v.scalar_tensor_tensor(out=gg2[:, d_], in0=gg[:, :, d_ * 4:(d_ + 1) * 4], scalar=-1.0,
                                                                        in1=ab[:, d_ * 4:(d_ + 1) * 4].unsqueeze(1).to_broadcast([128, NT, 4]),
                                                                        op0=ALU.mult, op1=ALU.mult), reads=[r_gg, r_ab], writes=[r_gg])
                yield
                bk, br = gps()
                bk2, br2 = gps()
                NH_ = NT * 4

                def mmg(t):
                    t.matmul(bk[:, 0:NH_], tri_f[:, 0, :], gg2[:, 0].rearrange("p t h -> p (t h)"), start=True, stop=True)
                    t.matmul(bk[:, NH_:2 * NH_], tri_f[:, 1, :], gg2[:, 1].rearrange("p t h -> p (t h)"), start=True, stop=True)
                    return t.matmul(bk2[:, 0:2 * NH_], ones_f[:], gg2[:].rearrange("p d t h -> p (d t h)"), start=True, stop=True)
                p.op("pe", mmg, reads=[r_gg, r_const], writes=[br, br2])
                yield
                p.op("dve", lambda v: v.tensor_copy(gcs[:].rearrange("p t (d h) -> p d t h", d=2), bk[:, 0:2 * NH_].rearrange("p (d t h) -> p d t h", d=2, h=4)),
                     reads=[br], writes=[r_gate])
                p.op("dve", lambda v: v.tensor_copy(gl[:].rearrange("p t (d h) -> p d t h", d=2), bk2[:, 0:2 * NH_].rearrange("p (d t h) -> p d t h", d=2, h=4)),
                     reads=[br2], writes=[r_gl])
                grel(bk, bk2)
                yield
                p.op("act", lambda a: a.activation(out=egc[:], in_=gcs[:], func=AF.Exp), reads=[r_gate], writes=[r_gate])
                p.op("act", lambda a: a.activation(out=egl[:], in_=gl[:], func=AF.Exp), reads=[r_gl], writes=[r_gate])
                yield
                p.op("dve", lambda v: v.tensor_tensor(out=gl[:], in0=gl[:], in1=gcs[:], op=ALU.subtract), reads=[r_gl, r_gate], writes=[r_gl])
                yield
                p.op("act", lambda a: a.activation(out=ekd[:], in_=gl[:], func=AF.Exp), reads=[r_gl], writes=[r_gate])
                p.op("dve", lambda v: v.tensor_tensor(out=bge[:], in0=beta[:], in1=egc[:], op=ALU.mult), reads=[r_gate], writes=[r_gate])
                yield
            with ExitStack() as s3:
                NP = LT + 8
                NA = LT + 4
                pad = [sb(s3, "pad%d" % i, [128, NP]) for i in range(3)]
                r_pad = [Res(), Res(), Res()]
                dgw_l = [sb(s3, "dgw%d" % i, [128, 5, 128]) for i in range(3)]; r_dgw_l = [Res(), Res(), Res()]
                sl_l = [sb(s3, "sl%d" % i, [128, NA]) for i in range(3)]; r_sl_l = [Res(), Res(), Res()]
                sq_l = [sb(s3, "sq%d" % i, [128, NA], BF16) for i in range(3)]; r_sq_l = [Res(), Res(), Res()]
                rinv_l = [sb(s3, "rinv%d" % i, [128, NA]) for i in range(3)]; r_rinv_l = [Res(), Res(), Res()]
                vTb_l = [sb(s3, "vTb%d" % i, [128, LT], BF16) for i in range(3)]; r_vTb_l = [Res(), Res(), Res()]
                cwrow = sb(s3, "cwrow", [5, 1536]); r_cwrow = Res()
                cw = sb(s3, "cw", [128, 12, 5]); r_cw = Res()
                stk = [sb(s3, "stk%d" % i, [128, 8, 128], BF16) for i in range(3)]; r_stk = [Res() for _ in range(3)]
                nstk = [0]
                p.op("pool", lambda g: (g.memset(pad[0][:], 0.0), g.memset(pad[1][:], 0.0), g.memset(pad[2][:], 0.0))[2], writes=r_pad)
                p.dma("sp", cwrow[:], conv_w[l], writes=[r_cwrow])
                bk, br = ps()

                def trc(t, bk=bk):
                    for c in range(12):
                        i = t.transpose(bk[:, c * 5:(c + 1) * 5], cwrow[0:5, c * 128:(c + 1) * 128], ident_f[0:5, 0:5])
                    return i
                p.op("pe", trc, reads=[r_cwrow, r_const], writes=[br])
                p.op("dve", lambda v, bk=bk: v.tensor_copy(cw[:].rearrange("p c k -> p (c k)"), bk[:, 0:60]), reads=[br], writes=[r_cw])

                def p3_chunk(ch):
                    kind, h = ch // 4, ch % 4
                    i2 = ch % 3
                    sl, r_sl, sq, r_sq, rinv, r_rinv = sl_l[i2], r_sl_l[i2], sq_l[i2], r_sq_l[i2], rinv_l[i2], r_rinv_l[i2]
                    P_, rP = pad[i2], r_pad[i2]
                    DG, rDG = dgw_l[i2], r_dgw_l[i2]
                    VT, rVT = vTb_l[i2], r_vTb_l[i2]
                    p.dma("sp", P_[:, 2:2 + LC], qkvT_s[ch * 128:(ch + 1) * 128, 0:LC], reads=[r_qkvT], writes=[rP])
                    p.dma("sp", P_[:, 6 + LC:6 + LT], qkvT_s[ch * 128:(ch + 1) * 128, LC:LT], reads=[r_qkvT], writes=[rP])
                    p.op("dve", lambda v: v.tensor_tensor(out=DG[:].bitcast(F32R), in0=ident_f[:].unsqueeze(1).to_broadcast([128, 5, 128]),
                                                          in1=cw[:, ch, :].unsqueeze(2).to_broadcast([128, 5, 128]), op=ALU.mult),
                         reads=[r_const, r_cw], writes=[rDG])
                    yield
                    a0 = 0
                    while a0 < NA:
                        gw = min(512, NA - a0)
                        bk, br = gps()

                        def mmc(tt, bk=bk, a0=a0, gw=gw):
                            for k in range(5):
                                i = tt.matmul(bk[:, 0:gw], DG[:, k, :].bitcast(F32R), P_[:, a0 + k:a0 + k + gw].bitcast(F32R),
                                              start=(k == 0), stop=(k == 4))
                            return i
                        p.op("pe", mmc, reads=[rP, rDG], writes=[br])
                        yield
                        p.op("act", lambda a, bk=bk, a0=a0, gw=gw: a.activation(out=sl[:, a0:a0 + gw], in_=bk[:, 0:gw], func=AF.Silu),
                             reads=[br], writes=[r_sl])
                        grel(bk)
                        a0 += 512
                    yield
                    if kind < 2:
                        p.op("dve", lambda g: g.tensor_tensor(out=sq[:], in0=sl[:], in1=sl[:], op=ALU.mult), reads=[r_sl], writes=[r_sq])
                        yield
                        a0 = 0
                        while a0 < NA:
                            gw = min(512, NA - a0)
                            bk, br = gps()
                            p.op("pe", lambda t, bk=bk, a0=a0, gw=gw: t.matmul(bk[:, 0:gw], ones_b[:], sq[:, a0:a0 + gw], start=True, stop=True),
                                 reads=[r_sq, r_const], writes=[br])
                            yield
                            p.op("dve", lambda v, bk=bk, a0=a0, gw=gw: v.tensor_scalar_add(rinv[:, a0:a0 + gw], bk[:, 0:gw], EPS),
                                 reads=[br], writes=[r_rinv])
                            grel(bk)
                            a0 += 512
                        yield
                        p.op("act", lambda a: a.activation(out=rinv[:], in_=rinv[:], func=AF.Ln), reads=[r_rinv], writes=[r_rinv])
                        p.op("act", lambda a: a.activation(out=rinv[:], in_=rinv[:], func=AF.Exp, scale=-0.5), reads=[r_rinv], writes=[r_rinv])
                        yield
                        dst, rdst = (qT, r_qT[h]) if kind == 0 else (kT, r_kT[h])
                        scl = (128.0 ** -0.5) if kind == 0 else 1.0
                        for (d0, d1, s0) in ((0, LC, 0), (LC, LT, LC + 4)):
                            p.op("dve", lambda v, d0=d0, d1=d1, s0=s0: v.scalar_tensor_tensor(
                                out=dst[:, h, d0:d1], in0=sl[:, s0:s0 + d1 - d0], scalar=scl, in1=rinv[:, s0:s0 + d1 - d0],
                                op0=ALU.mult, op1=ALU.mult), reads=[r_sl, r_rinv], writes=[rdst])
                        if kind == 1:
                            srcT, rsrc, dtok, rdt = kT[:, h, :], r_kT[h], ktok_s, r_ktok_s
                    else:
                        p.op("act", lambda a: a.copy(VT[:, 0:LC], sl[:, 0:LC]), reads=[r_sl], writes=[rVT])
                        p.op("act", lambda a: a.copy(VT[:, LC:LT], sl[:, LC + 4:LT + 4]), reads=[r_sl], writes=[rVT])
                        srcT, rsrc, dtok, rdt = VT[:, :], rVT, vtok_s, r_vtok_s
                    yield
                    if kind >= 1:
                        for t0 in range(0, NT, 8):
                            n = min(8, NT - t0)
                            bk, br = gps()
                            bkb = bk[:].bitcast(BF16)

                            def trk(tt, bkb=bkb, t0=t0, n=n):
                                for i in range(n):
                                    ii = tt.transpose(bkb[:, i * 128:(i + 1) * 128], srcT[:, (t0 + i) * 128:(t0 + i + 1) * 128], ident_b[:])
                                return ii
                            p.op("pe", trk, reads=[rsrc, r_const], writes=[br])
                            yield
                            SK, rSK = stk[nstk[0] % 3], r_stk[nstk[0] % 3]
                            nstk[0] += 1
                            p.op("act", lambda a, bkb=bkb, n=n, SK=SK: a.copy(SK[:, 0:n, :], bkb[:, 0:n * 128].rearrange("p (t d) -> p t d", d=128)),
                                 reads=[br], writes=[rSK])
                            grel(bk)
                            p.dma("sp", dtok[t0:t0 + n, :, h, :].rearrange("t p d -> p t d"), SK[:, 0:n, :], reads=[rSK], writes=[rdt], key=rSK)
                            yield
                run_pipelined([gates_gen()] + [p3_chunk(ch) for ch in range(12)], 3, stagger=8)
                p.barrier()
            sGt.close()
            dbg_dump("qT", qT[:], r_qT, [128, 4, LT])
            dbg_dump("kT", kT[:], r_kT, [128, 4, LT])
            if dbg and "gates" in dbg and (b, l) == dbg_bl:
                with ExitStack() as sd:
                    d32 = sb(sd, "d32g", [128, NT, 4, 8]); r_d = Res()
                    for i, tl in enumerate((beta, gcs, egc, ekd)):
                        p.op("dve", lambda v, i=i, tl=tl: v.tensor_copy(d32[:, :, i, :], tl[:]), reads=[r_gate], writes=[r_d])
                    p.dma("sp", dbg_out["gates"], d32[:], reads=[r_d])
                    p.barrier()

            sAtt = ExitStack()
            KT = sb(sAtt, "KT", [64, 2, LT], BF16); r_KT = [Res() for _ in range(NT)]
            V1 = sb(sAtt, "V1", [128, NT, 2, 65], BF16); r_V1 = [Res() for _ in range(NT)]
            QTt = [sb(sAtt, "QTt%d" % i, [64, 8, 128], BF16) for i in range(3)]; r_QTt = [Res() for _ in range(3)]
            nwqk = sb(sAtt, "nwqk", [128, 10, 64]); r_nw2 = Res()
            esink = sb(sAtt, "esink", [128, 8]); r_esink = Res()
            p.dma("sp", nwqk[:, 0:8, :], q_norm_w[l:l + 1, :].unsqueeze(1).to_broadcast([128, 8, 64]), writes=[r_nw2])
            p.dma("sp", nwqk[:, 8:10, :], k_norm_w[l:l + 1, :].unsqueeze(1).to_broadcast([128, 2, 64]), writes=[r_nw2])
            p.dma("sp", esink[:], sink[l:l + 1, :].to_broadcast([128, 8]), writes=[r_esink])
            p.op("act", lambda a: a.activation(out=esink[:], in_=esink[:], func=AF.Exp), reads=[r_esink], writes=[r_esink])
            p.op("pool", lambda g: g.memset(V1[:], 1.0), writes=r_V1)
            qk = [sb(sAtt, "qk%d" % i, [128, 10, 64]) for i in range(2)]; r_qk = [Res(), Res()]
            vv = [sb(sAtt, "vv%d" % i, [128, 2, 64]) for i in range(2)]; r_vv = [Res(), Res()]
            sq5_l = [sb(sAtt, "sq5%d" % i, [128, 10, 64]) for i in range(2)]; r_sq5_l = [Res(), Res()]
            s10_l = [sb(sAtt, "s10%d" % i, [128, 10]) for i in range(2)]; r_s10_l = [Res(), Res()]
            kn_l = [sb(sAtt, "kn%d" % i, [128, 10, 64]) for i in range(2)]; r_kn_l = [Res(), Res()]
            ra_l = [sb(sAtt, "ra%d" % i, [128, 10, 2, 16]) for i in range(2)]; r_ra_l = [Res() for _ in range(2)]
            rb_l = [sb(sAtt, "rb%d" % i, [128, 10, 2, 16]) for i in range(2)]; r_rb_l = [Res() for _ in range(2)]
            qkr_l = [sb(sAtt, "qkr%d" % i, [128, 10, 64], BF16) for i in range(2)]; r_qkr_l = [Res(), Res()]
            Pt = sb(sAtt, "Pt", [128, 5, 4, 128], BF16); r_Pt = [Res() for _ in range(5)]
            zb = [sb(sAtt, "zb%d" % i, [128, 8, 64]) for i in range(2)]; r_zb = [Res(), Res()]
            den = sb(sAtt, "den", [128, 4]); r_den = Res()
            ob = sb(sAtt, "ob", [128, 4, 64]); r_ob = Res()
            ybt_l = [sb(sAtt, "ybt%d" % i, [128, 8, 64], BF16) for i in range(2)]; r_ybt_l = [Res(), Res()]

            def ld5(t):
                i = t % 2
                p.dma("sp", qk[i][:, 8:10, :].rearrange("p h d -> p (h d)"), tokp_s[t * 128:(t + 1) * 128, TP_KB:TP_KB + 128],
                      reads=[r_tokp], writes=[r_qk[i]])
                p.dma("sp", vv[i][:].rearrange("p h d -> p (h d)"), tokp_s[t * 128:(t + 1) * 128, TP_VB:TP_VB + 128],
                      reads=[r_tokp], writes=[r_vv[i]])
                if t >= t_lo:
                    p.dma("sp", qk[i][:, 0:8, :].rearrange("p h d -> p (h d)"), tokp_s[t * 128:(t + 1) * 128, TP_QB:TP_QB + 512],
                          reads=[r_tokp], writes=[r_qk[i]])
                else:
                    p.op("pool", lambda g, i=i: g.memset(qk[i][:, 0:8, :], 0.0), writes=[r_qk[i]])

            def ldzb(t):
                p.dma("sp", zb[t % 2][:].rearrange("p h d -> p (h d)"), tokp_s[t * 128:(t + 1) * 128, TP_ZB:TP_ZB + 512],
                      reads=[r_tokp], writes=[r_zb[t % 2]])

            def att_prep(t):
                i = t % 2
                Q_ = qk[i]
                sq5, r_sq5, s10, r_s10, kn, r_kn, qkr, r_qkr = sq5_l[i], r_sq5_l[i], s10_l[i], r_s10_l[i], kn_l[i], r_kn_l[i], qkr_l[i], r_qkr_l[i]
                ra, r_ra, rb_, r_rb = ra_l[0], r_ra_l[0], rb_l[0], r_rb_l[0]
                ra2, r_ra2, rb2, r_rb2 = ra_l[1], r_ra_l[1], rb_l[1], r_rb_l[1]
                p.op("act", lambda a: a.copy(V1[:, t, :, 0:64], vv[i][:]), reads=[r_vv[i]], writes=[r_V1[t]])
                p.op("pool", lambda g: g.tensor_tensor(out=sq5[:], in0=Q_[:], in1=Q_[:], op=ALU.mult), reads=[r_qk[i]], writes=[r_sq5])
                yield
                p.op("dve", lambda v: v.reduce_sum(out=s10[:], in_=sq5[:], axis=AX.X), reads=[r_sq5], writes=[r_s10])
                p.op("dve", lambda v: v.tensor_scalar(out=s10[:], in0=s10[:], scalar1=1.0 / 64, scalar2=EPS, op0=ALU.mult, op1=ALU.add),
                     reads=[r_s10], writes=[r_s10])
                yield
                p.op("act", lambda a: a.activation(out=s10[:], in_=s10[:], func=AF.Sqrt), reads=[r_s10], writes=[r_s10])
                yield
                p.op("dve", lambda v: v.reciprocal(s10[:], s10[:]), reads=[r_s10], writes=[r_s10])
                p.op("dve", lambda v: v.tensor_tensor(out=kn[:], in0=Q_[:], in1=s10[:].unsqueeze(2).to_broadcast([128, 10, 64]), op=ALU.mult),
                     reads=[r_qk[i], r_s10], writes=[r_kn])
                yield
                if t < NTC:
                    p.op("dve", lambda v: v.tensor_tensor(out=qkr[:], in0=kn[:], in1=nwqk[:], op=ALU.mult), reads=[r_kn, r_nw2], writes=[r_qkr])
                else:
                    p.op("pool", lambda v: v.tensor_tensor(out=kn[:], in0=kn[:], in1=nwqk[:], op=ALU.mult), reads=[r_kn, r_nw2], writes=[r_kn])
                    yield
                    n = t - NTC
                    k5 = kn[:].rearrange("p h (a f j) -> p h a f j", a=2, f=2)
                    o5 = qkr[:].rearrange("p h (a f j) -> p h a f j", a=2, f=2)
                    cosb = rope_t[:, n, 0, :, :].unsqueeze(1).to_broadcast([128, 10, 2, 16])
                    sinb = rope_t[:, n, 1, :, :].unsqueeze(1).to_broadcast([128, 10, 2, 16])
                    x1 = k5[:, :, :, 0, :]
                    x2 = k5[:, :, :, 1, :]
                    p.op("dve", lambda v: v.tensor_tensor(out=ra[:], in0=x1, in1=cosb, op=ALU.mult), reads=[r_kn, r_const], writes=[r_ra])
                    p.op("pool", lambda g: g.tensor_tensor(out=rb_[:], in0=x2, in1=sinb, op=ALU.mult), reads=[r_kn, r_const], writes=[r_rb])
                    p.op("dve", lambda v: v.tensor_tensor(out=ra2[:], in0=x2, in1=cosb, op=ALU.mult), reads=[r_kn, r_const], writes=[r_ra2])
                    p.op("pool", lambda g: g.tensor_tensor(out=rb2[:], in0=x1, in1=sinb, op=ALU.mult), reads=[r_kn, r_const], writes=[r_rb2])
                    yield
                    p.op("dve", lambda v: v.tensor_tensor(out=o5[:, :, :, 0, :], in0=ra[:], in1=rb_[:], op=ALU.subtract), reads=[r_ra, r_rb], writes=[r_qkr])
                    p.op("pool", lambda v: v.tensor_tensor(out=o5[:, :, :, 1, :], in0=ra2[:], in1=rb2[:], op=ALU.add), reads=[r_ra2, r_rb2], writes=[r_qkr])
                yield
                bk, br = gps()
                bkb = bk[:].bitcast(BF16)

                def tr5q(tt):
                    for h in range(8):
                        ii = tt.transpose(bkb[0:64, h * 128:(h + 1) * 128], qkr[:, h, :], ident_b[:])
                    return ii
                p.op("pe", tr5q, reads=[r_qkr, r_const], writes=[br])
                yield
                p.op("act", lambda a: a.copy(QTt[t % 3][:], bkb[0:64, :].rearrange("p (h n) -> p h n", h=8)), reads=[br], writes=[r_QTt[t % 3]])
                grel(bk)
                bk2, br2 = gps()
                bkb2 = bk2[:].bitcast(BF16)

                def tr5k(tt):
                    for h in range(2):
                        ii = tt.transpose(bkb2[0:64, h * 128:(h + 1) * 128], qkr[:, 8 + h, :], ident_b[:])
                    return ii
                p.op("pe", tr5k, reads=[r_qkr, r_const], writes=[br2])
                yield
                p.op("act", lambda a: a.copy(KT[:, :, t * 128:(t + 1) * 128], bkb2[0:64, 0:256].rearrange("p (h n) -> p h n", h=2)),
                     reads=[br2], writes=[r_KT[t]])
                grel(bk2)
                yield

            def att_main(tq):
                if tq < NTC:
                    keys = [(0, None), (1, None)]
                else:
                    keys = []
                    if tq - 1 >= NTC:
                        keys.append((tq - 1, mk_prev))
                    keys.append((tq, None))
                    if tq + 1 < NT:
                        keys.append((tq + 1, mk_next))
                    keys += [(0, None), (1, None)]
                ybt, r_ybt = ybt_l[tq % 2], r_ybt_l[tq % 2]
                QX, rQX = QTt[tq % 3], r_QTt[tq % 3]
                for hk in range(2):
                    for i, (kt, mk) in enumerate(keys):
                        bS, rS = gps()
                        p.op("pe", lambda tt, bS=bS, kt=kt: tt.matmul(
                            bS[:].rearrange("p (g n) -> p g n", g=4), KT[:, hk, kt * 128:(kt + 1) * 128],
                            QX[:, hk * 4:(hk + 1) * 4, :], start=True, stop=True),
                            reads=[r_KT[kt], rQX], writes=[rS])
                        yield
                        p.op("act", lambda a, bS=bS, i=i: a.activation(out=Pt[:, i, :, :], in_=bS[:].rearrange("p (g n) -> p g n", g=4),
                                                                       func=AF.Exp, scale=0.125), reads=[rS], writes=[r_Pt[i]])
                        grel(bS)
                        if mk is not None:
                            p.op("pool", lambda v, i=i, mk=mk: v.tensor_tensor(out=Pt[:, i, :, :], in0=Pt[:, i, :, :],
                                                                              in1=mk[:].unsqueeze(1).to_broadcast([128, 4, 128]), op=ALU.mult),
                                 reads=[r_Pt[i], r_const], writes=[r_Pt[i]])
                        yield
                    bO, rO = gps()

                    def mmo(tt, bO=bO, hk=hk):
                        for g in range(4):
                            for i, (kt, mk) in enumerate(keys):
                                ii = tt.matmul(bO[:, g * 65:(g + 1) * 65], Pt[:, i, g, :], V1[:, kt, hk, :],
                                               start=(g == 0 and i == 0), stop=(i == len(keys) - 1), skip_group_check=True)
                        return ii
                    p.op("pe", mmo, reads=r_Pt[0:len(keys)] + [r_V1[kt] for kt, _ in keys], writes=[rO])
                    yield
                    bOv = bO[:, 0:260].rearrange("p (g d) -> p g d", g=4)
                    p.op("dve", lambda v, bOv=bOv, hk=hk: v.tensor_tensor(out=den[:], in0=bOv[:, :, 64], in1=esink[:, hk * 4:(hk + 1) * 4], op=ALU.add),
                         reads=[rO, r_esink], writes=[r_den])
                    p.op("dve", lambda v: v.reciprocal(den[:], den[:]), reads=[r_den], writes=[r_den])
                    p.op("dve", lambda v, bOv=bOv: v.tensor_tensor(out=ob[:], in0=bOv[:, :, 0:64], in1=den[:].unsqueeze(2).to_broadcast([128, 4, 64]), op=ALU.mult),
                         reads=[rO, r_den], writes=[r_ob])
                    grel(bO)
                    p.op("pool", lambda v, hk=hk: v.tensor_tensor(out=ybt[:, hk * 4:(hk + 1) * 4, :], in0=ob[:], in1=zb[tq % 2][:, hk * 4:(hk + 1) * 4, :], op=ALU.mult),
                         reads=[r_ob, r_zb[tq % 2]], writes=[r_ybt])
                    yield
                bY, rY = gps()
                bYb = bY[:].bitcast(BF16).rearrange("p (h j) -> p h j", h=8)
                ybt2 = ybt[:].rearrange("p h d -> p (h d)")

                def try_(tt):
                    for c in range(4):
                        ii = tt.transpose(bYb[:, c, :], ybt2[:, c * 128:(c + 1) * 128], ident_b[:])
                    return ii
                p.op("pe", try_, reads=[r_ybt, r_const], writes=[rY])
                yield
                p.op("act", lambda a: a.copy(ybT[:, :, tq * 128:(tq + 1) * 128], bYb[:, 0:4, :]), reads=[rY], writes=[r_ybT[tq]])
                grel(bY)
                yield

            def att_chain():
                ld5(0)
                ldzb(t_lo)
                for t in range(NT):
                    if t + 1 < NT:
                        ld5(t + 1)
                    yield from att_prep(t)
                    tq = t - 1
                    if tq >= t_lo:
                        if tq + 1 < NT:
                            ldzb(tq + 1)
                        yield from att_main(tq)
                if NT - 1 >= t_lo:
                    yield from att_main(NT - 1)

            with ExitStack() as s4o:
                Sst = sb(s4o, "Sst", [128, 2, 4, 128]); r_S = [Res(), Res()]
                s4 = ExitStack()
                Sb_ = sb(s4, "Sb", [128, 2, 4, 128], BF16); r_Sb = [Res(), Res()]
                p.op("pool", lambda g: (g.memset(Sst[:], 0.0), g.memset(Sb_[:], 0.0))[1], writes=r_S + r_Sb)

                class TS:
                    pass

                def mk_ts(d):
                    T_ = TS()
                    for nm, shp, dt in (("dg", [128, 4, 128], F32), ("mg", [128, 4, 128], F32), ("gcb", [128, 4, 128], F32), ("E_", [128, 4, 128], F32),
                                        ("egr", [128, 4, 128], BF16), ("W1", [128, 4, 128], BF16),
                                        ("NQ", [128, 8, 128], BF16), ("NQT", [128, 8, 128], BF16), ("X0", [128, 4, 256], BF16),
                                        ("wb", [128, 4, 128], BF16), ("wTn", [128, 4, 128], BF16), ("vnew", [128, 4, 128], BF16),
                                        ("kd", [128, 4, 128], BF16), ("qdT", [128, 4, 128], BF16), ("M1s", [128, 2, 4, 128], BF16)):
                        setattr(T_, nm, sb(s4, "%s_d%d" % (nm, d), shp, dt))
                        setattr(T_, "r_" + nm, Res())
                    T_.TT = sb(s4, "TT_d%d" % d, [128, 2, 4, 128], BF16)
                    T_.r_TT = Res()
                    T_.kv = [sb(s4, "kv%d_d%d" % (i, d), [128, 2, 4, 128], BF16) for i in range(2)]
                    T_.r_kv = [Res(), Res()]
                    T_.ost = [sb(s4, "ost%d_d%d" % (i, d), [128, 4, 128]) for i in range(2)]
                    T_.r_ost = [Res(), Res()]
                    T_.nstep = 0
                    return T_
                TSS = [mk_ts(0), mk_ts(1)]

                def bc_h(ap2):
                    return ap2.unsqueeze(1).to_broadcast([128, 4, 128])

                def bc_j(ap2):
                    return ap2.unsqueeze(2).to_broadcast([128, 4, 128])

                def v4(bank):
                    return bank[:].rearrange("p (h j) -> p h j", h=4)

                def ld_kv(d, t, slot):
                    T_ = TSS[d]
                    p.dma("sp", T_.kv[slot][:, 0], ktok_s[t], reads=[r_ktok_s], writes=[T_.r_kv[slot]])
                    p.dma("sp", T_.kv[slot][:, 1], vtok_s[t], reads=[r_vtok_s], writes=[T_.r_kv[slot]])

                def gdn_step(d, t, with_out, t_next):
                    T_ = TSS[d]
                    slot = T_.nstep % 2
                    T_.nstep += 1
                    if t_next is not None:
                        ld_kv(d, t_next, 1 - slot)
                    ktk = T_.kv[slot][:, 0]
                    vtk = T_.kv[slot][:, 1]
                    r_kvs = [T_.r_kv[slot]]
                    dg, mg, E_, egr, W1, NQ, NQT, X0 = T_.dg, T_.mg, T_.E_, T_.egr, T_.W1, T_.NQ, T_.NQT, T_.X0
                    tA = T_.gcb
                    T_.r_tA = T_.r_gcb
                    wb, wTn, vnew, kd, qdT, M1s = T_.wb, T_.wTn, T_.vnew, T_.kd, T_.qdT, T_.M1s
                    dh = slice(d * 4, d * 4 + 4)
                    ck = slice(t * 128, (t + 1) * 128)
                    rd_qk = r_qT + r_kT
                    p.op("dve", lambda g: g.tensor_tensor(out=dg[:], in0=bc_h(ident_f[:]), in1=bc_j(gcs[:, t, dh]), op=ALU.mult),
                         reads=[r_const, r_gate], writes=[T_.r_dg])
                    p.op("pool", lambda g: g.tensor_tensor(out=mg[:], in0=bc_h(m_incl[:, d, :]), in1=bc_j(gcs[:, t, dh]), op=ALU.subtract),
                         reads=[r_const, r_gate], writes=[T_.r_mg])
                    p.op("pool", lambda g: g.tensor_tensor(out=W1[:], in0=bc_h(m_strict[:, d, :]), in1=bc_j(beta[:, t, dh]), op=ALU.mult),
                         reads=[r_const, r_gate], writes=[T_.r_W1])
                    def late_pool(which):
                        if which == 0:
                            p.op("pool", lambda g: g.tensor_tensor(out=X0[:, :, 0:128], in0=vtk, in1=bc_j(beta[:, t, dh]), op=ALU.mult),
                                 reads=r_kvs + [r_gate], writes=[T_.r_X0])
                        elif which == 1:
                            p.op("pool", lambda g: g.tensor_tensor(out=X0[:, :, 128:256], in0=ktk, in1=bc_j(bge[:, t, dh]), op=ALU.mult),
                                 reads=r_kvs + [r_gate], writes=[T_.r_X0])
                        else:
                            p.op("pool", lambda g: g.tensor_tensor(out=kd[:], in0=ktk, in1=bc_j(ekd[:, t, dh]), op=ALU.mult),
                                 reads=r_kvs + [r_gate], writes=[T_.r_kd])
                    yield
                    bB, rB = gps()

                    def mm1(tt):
                        return tt.matmul(bB[:], ones_f[:], dg[:].rearrange("p h j -> p (h j)"), start=True, stop=True)
                    p.op("pe", mm1, reads=[r_const, T_.r_dg], writes=[rB])
                    yield
                    p.op("dve", lambda v: v.tensor_tensor(out=T_.gcb[:], in0=v4(bB), in1=mg[:], op=ALU.add), reads=[rB, T_.r_mg], writes=[T_.r_gcb])
                    if with_out:
                        p.op("act", lambda a: a.activation(out=egr[:], in_=v4(bB), func=AF.Exp), reads=[rB, T_.r_gcb], writes=[T_.r_egr])
                    p.op("act", lambda a: a.activation(out=E_[:], in_=T_.gcb[:], func=AF.Exp, scale=-1.0), reads=[T_.r_gcb], writes=[T_.r_E_])
                    grel(bB)
                    pK, rKQ, iK = gps2()
                    pKv = pK.rearrange("p w (h j) -> p w h j", h=4)

                    def mm2(tt):
                        for h in range(4):
                            tt.matmul(pKv[:, 0, h, :], kT[:, h, ck], kT[:, h, ck], start=True, stop=True)
                        for h in range(4):
                            i = tt.matmul(pKv[:, 1, h, :], qT[:, h, ck], kT[:, h, ck], start=True, stop=True)
                        return i
                    p.op("pe", mm2, reads=rd_qk, writes=list(rKQ))
                    p.op("pool", lambda g: g.tensor_copy(T_.TT[:].rearrange("p w h j -> p (w h) j"), ident_b[:].unsqueeze(1).to_broadcast([128, 8, 128])),
                         reads=[r_const], writes=[T_.r_TT])
                    yield
                    p.op("dve", lambda v: v.tensor_tensor(out=NQ[:].rearrange("p (w h) j -> p w h j", w=2), in0=pKv,
                                                          in1=E_[:].unsqueeze(1).to_broadcast([128, 2, 4, 128]), op=ALU.mult),
                         reads=list(rKQ) + [T_.r_E_], writes=[T_.r_NQ])
                    p.op("dve", lambda v: v.tensor_tensor(out=NQ[:, 0:4, :], in0=NQ[:, 0:4, :], in1=W1[:], op=ALU.mult), reads=[T_.r_NQ, T_.r_W1], writes=[T_.r_NQ])
                    grel2(iK)
                    yield
                    bT, rT = gps()
                    bTb = bT[:].bitcast(BF16).rearrange("p (h j) -> p h j", h=8)

                    def tr1(tt):
                        for i in range(8):
                            ii = tt.transpose(bTb[:, i, :], NQ[:, i, :], ident_b[:])
                        return ii
                    p.op("pe", tr1, reads=[T_.r_NQ, r_const], writes=[rT])
                    yield
                    p.op("act", lambda a: a.copy(NQT[:], bTb), reads=[rT], writes=[T_.r_NQT])
                    grel(bT)

                    yield
                    TT, rTT_ = T_.TT, T_.r_TT
                    um2 = gmask[:, d].bitcast(mybir.dt.uint16)

                    def m2(li):
                        return um2[:, li, :, :].unsqueeze(2).to_broadcast([128, 2, 4, 128])
                    p.op("dve", lambda v: v.copy_predicated(TT[:, 0], bc_h(um2[:, 0, 0, :]), NQ[:, 0:4, :]), reads=[T_.r_NQ, r_const, rTT_], writes=[rTT_])
                    p.op("dve", lambda v: v.copy_predicated(TT[:, 1], bc_h(um2[:, 0, 1, :]), NQT[:, 0:4, :]), reads=[T_.r_NQT, r_const, rTT_], writes=[rTT_])
                    yield
                    for li in range(1, 7):
                        if li <= 3:
                            late_pool(li - 1)
                        elif li == 4 and with_out:
                            p.op("pool", lambda g: g.tensor_tensor(out=qdT[:], in0=qT[:, :, ck], in1=egr[:], op=ALU.mult),
                                 reads=r_qT + [T_.r_egr], writes=[T_.r_qdT])
                        pA, rA, iA = gps2()
                        pAv = pA.rearrange("p w (h j) -> p w h j", h=4)

                        def mmA(tt, pAv=pAv):
                            for h in range(4):
                                tt.matmul(pAv[:, 0, h, :], NQT[:, h, :], TT[:, 0, h, :], start=True, stop=True)
                            for h in range(4):
                                i = tt.matmul(pAv[:, 1, h, :], NQ[:, h, :], TT[:, 1, h, :], start=True, stop=True)
                            return i
                        p.op("pe", mmA, reads=[T_.r_NQ, T_.r_NQT, rTT_], writes=list(rA))
                        yield
                        p.op("act", lambda a, pAv=pAv: a.copy(M1s[:], pAv), reads=list(rA), writes=[T_.r_M1s])
                        grel2(iA)
                        yield
                        pB, rB2_, iB = gps2()
                        pBv = pB.rearrange("p w (h j) -> p w h j", h=4)

                        def mmB(tt, pBv=pBv):
                            for h in range(4):
                                tt.matmul(pBv[:, 0, h, :], TT[:, 1, h, :], M1s[:, 0, h, :], start=True, stop=True)
                            for h in range(4):
                                i = tt.matmul(pBv[:, 1, h, :], TT[:, 0, h, :], M1s[:, 1, h, :], start=True, stop=True)
                            return i
                        p.op("pe", mmB, reads=[T_.r_M1s, rTT_], writes=list(rB2_))
                        yield
                        p.op("dve", lambda v, pBv=pBv, li=li: v.copy_predicated(TT[:], m2(li), pBv), reads=list(rB2_) + [r_const], writes=[rTT_])
                        grel2(iB)
                        yield
                    pX, rX, iX = gps2()
                    pXv = pX.rearrange("p w (h j) -> p w h j", h=2)

                    def xv(h):
                        return pX[:, h // 2, (h % 2) * 256:(h % 2) * 256 + 256]

                    def mm3(tt):
                        for h in range(4):
                            i = tt.matmul(xv(h), TT[:, 1, h, :], X0[:, h, :], start=(h % 2 == 0), stop=True, skip_group_check=True)
                        return i
                    p.op("pe", mm3, reads=[T_.r_X0, rTT_], writes=list(rX))
                    yield
                    p.op("act", lambda a: a.copy(wb[:].rearrange("p (w h) j -> p w h j", w=2), pXv[:, :, :, 128:256]),
                         reads=list(rX), writes=[T_.r_wb])
                    yield
                    bW, rW = gps()
                    bWb = bW[:].bitcast(BF16).rearrange("p (h j) -> p h j", h=8)

                    def tr2(tt):
                        for h in range(4):
                            ii = tt.transpose(bWb[:, h, :], wb[:, h, :], ident_b[:])
                        return ii
                    p.op("pe", tr2, reads=[T_.r_wb, r_const], writes=[rW])
                    yield
                    p.op("dve", lambda v: v.tensor_scalar(out=wTn[:], in0=bWb[:, 0:4, :], scalar1=-1.0, scalar2=None, op0=ALU.mult),
                         reads=[rW], writes=[T_.r_wTn])
                    grel(bW)
                    yield

                    def mm6(tt):
                        for h in range(4):
                            i = tt.matmul(xv(h)[:, 0:128], wTn[:, h, :], Sb_[:, d, h, :], start=False, stop=True, skip_group_check=True)
                        return i
                    p.op("pe", mm6, reads=[T_.r_wTn, r_Sb[d]], writes=list(rX))
                    yield
                    p.op("act", lambda a: a.copy(vnew[:].rearrange("p (w h) j -> p w h j", w=2), pXv[:, :, :, 0:128]),
                         reads=list(rX), writes=[T_.r_vnew])
                    grel2(iX)
                    yield
                    bS, rS = gps()

                    def mm8(tt):
                        for h in range(4):
                            i = tt.matmul(v4(bS)[:, h, :], kd[:, h, :], vnew[:, h, :], start=True, stop=True)
                        return i
                    p.op("pe", mm8, reads=[T_.r_kd, T_.r_vnew], writes=[rS])
                    if with_out:
                        bO, rO = gps()

                        def mm7(tt):
                            for h in range(4):
                                tt.matmul(v4(bO)[:, h, :], qdT[:, h, :], Sb_[:, d, h, :], start=True, stop=False)
                                i = tt.matmul(v4(bO)[:, h, :], NQT[:, 4 + h, :], vnew[:, h, :], start=False, stop=True)
                            return i
                        p.op("pe", mm7, reads=[T_.r_qdT, r_Sb[d], T_.r_NQT, T_.r_vnew], writes=[rO])
                    p.op("pool", lambda g: g.tensor_tensor(out=Sst[:, d, :, :], in0=Sst[:, d, :, :], in1=bc_j(egl[:, t, dh]), op=ALU.mult),
                         reads=[r_S[d], r_gate], writes=[r_S[d]])
                    yield
                    p.op("dve", lambda v: v.tensor_tensor(out=Sst[:, d, :, :], in0=Sst[:, d, :, :], in1=v4(bS), op=ALU.add),
                         reads=[r_S[d], rS], writes=[r_S[d]])
                    p.op("act", lambda a: a.copy(Sb_[:, d, :, :], Sst[:, d, :, :]), reads=[r_S[d]], writes=[r_Sb[d]])
                    if with_out:
                        OS, rOS = T_.ost[slot], T_.r_ost[slot]
                        p.op("act", lambda a: a.copy(OS[:], v4(bO)), reads=[rO], writes=[rOS])
                        p.dma("act", o_s[d, t * 128:(t + 1) * 128, :], OS[:].rearrange("p h j -> p (h j)"), reads=[rOS], writes=[r_o_s[d][t]], key=rOS)
                        grel(bO)
                    grel(bS)
                    yield

                def chain(d, order):
                    ld_kv(d, order[0], 0)
                    for i_, t in enumerate(order):
                        yield from gdn_step(d, t, upd_ctx or t >= NTC, order[i_ + 1] if i_ + 1 < len(order) else None)

                gens = [chain(0, list(range(0, NT))), chain(1, [1, 0] + list(range(NT - 1, NTC - 1, -1))), att_chain()]
                alive = [True, True, True]
                rnd = 0
                for _ in range(GDN_PHASE):
                    next(gens[0])
                while any(alive):
                    rnd += 1
                    for gi_ in (0, 2, 1):
                        if alive[gi_]:
                            try:
                                next(gens[gi_])
                            except StopIteration:
                                alive[gi_] = False
                nrot[0] = 8
                dbg_dump("Sfin", Sst[:], r_S, [128, 2, 4, 128], direct=True)
                p.barrier()
                s4.close()
            sAtt.close()

        stY = ExitStack()
        yaT = sb(stY, "yaT", [128, 4, LT], BF16); r_yaT = [Res() for _ in range(NT)]

        def bc_h(ap2):
            return ap2.unsqueeze(1).to_broadcast([128, 4, 128])

        def bc_j(ap2):
            return ap2.unsqueeze(2).to_broadcast([128, 4, 128])
        s6 = ExitStack()
        if True:
            gnw = sb(s6, "gnw", [128, 128]); r_gnw = Res()
            D6 = 3
            of = [sb(s6, "of%d" % i, [128, 4, 128]) for i in range(D6)]; r_of = [Res() for _ in range(D6)]
            ob6 = [sb(s6, "ob6%d" % i, [128, 4, 128]) for i in range(D6)]; r_ob6 = [Res() for _ in range(D6)]
            za = [sb(s6, "za%d" % i, [128, 4, 128]) for i in range(D6)]; r_za = [Res() for _ in range(D6)]
            sq6_l = [sb(s6, "sq6%d" % i, [128, 4, 128]) for i in range(D6)]; r_sq6_l = [Res() for _ in range(D6)]
            s4t_l = [sb(s6, "s4t%d" % i, [128, 4]) for i in range(D6)]; r_s4_l = [Res() for _ in range(D6)]
            t1_l = [sb(s6, "t1%d" % i, [128, 4, 128]) for i in range(D6)]; r_t1_l = [Res() for _ in range(D6)]
            t2_l = [sb(s6, "t2%d" % i, [128, 4, 128]) for i in range(D6)]; r_t2_l = [Res() for _ in range(D6)]
            yab_l = [sb(s6, "yab%d" % i, [128, 4, 128], BF16) for i in range(D6)]; r_yab_l = [Res() for _ in range(D6)]
            p.dma("sp", gnw[:], gdn_norm_w[l:l + 1, :].to_broadcast([128, 128]), writes=[r_gnw])

            def p6_tile(t):
                i = (t - t_lo) % D6
                sq6, r_sq6, s4t, r_s4, t1, r_t1, t2, r_t2, yab, r_yab = (sq6_l[i], r_sq6_l[i], s4t_l[i], r_s4_l[i], t1_l[i], r_t1_l[i],
                                                                          t2_l[i], r_t2_l[i], yab_l[i], r_yab_l[i])
                p.dma("sp", za[i][:].rearrange("p h j -> p (h j)"), tokp_s[t * 128:(t + 1) * 128, TP_ZA:TP_ZA + 512],
                      reads=[r_tokp], writes=[r_za[i]])
                p.dma("sp", of[i][:].rearrange("p h j -> p (h j)"), o_s[0, t * 128:(t + 1) * 128, :], reads=[r_o_s[0][t]], writes=[r_of[i]])
                p.dma("sp", ob6[i][:].rearrange("p h j -> p (h j)"), o_s[1, t * 128:(t + 1) * 128, :], reads=[r_o_s[1][t]], writes=[r_ob6[i]])
                yield
                O = of[i][:]
                p.op("pool", lambda g: g.tensor_tensor(out=O, in0=O, in1=ob6[i][:], op=ALU.add), reads=[r_of[i], r_ob6[i]], writes=[r_of[i]])
                p.op("pool", lambda g: g.tensor_tensor(out=sq6[:], in0=O, in1=O, op=ALU.mult), reads=[r_of[i]], writes=[r_sq6])
                p.op("pool", lambda g: g.tensor_tensor(out=t2[:], in0=za[i][:], in1=bc_h(gnw[:]), op=ALU.mult), reads=[r_za[i], r_gnw], writes=[r_t2])
                yield
                p.op("dve", lambda v: v.reduce_sum(out=s4t[:], in_=sq6[:], axis=AX.X), reads=[r_sq6], writes=[r_s4])
                p.op("dve", lambda v: v.tensor_scalar(out=s4t[:], in0=s4t[:], scalar1=1.0 / 128, scalar2=EPS, op0=ALU.mult, op1=ALU.add),
                     reads=[r_s4], writes=[r_s4])
                yield
                p.op("act", lambda a: a.activation(out=s4t[:], in_=s4t[:], func=AF.Sqrt), reads=[r_s4], writes=[r_s4])
                yield
                p.op("dve", lambda v: v.reciprocal(s4t[:], s4t[:]), reads=[r_s4], writes=[r_s4])
                p.op("dve", lambda v: v.tensor_tensor(out=t1[:], in0=O, in1=bc_j(s4t[:]), op=ALU.mult), reads=[r_of[i], r_s4], writes=[r_t1])
                p.op("dve", lambda v: v.tensor_tensor(out=yab[:], in0=t1[:], in1=t2[:], op=ALU.mult), reads=[r_t1, r_t2], writes=[r_yab])
                yield
                bk, br = gps()
                bkb = bk[:].bitcast(BF16).rearrange("p (h j) -> p h j", h=8)

                def tr6(tt):
                    for h in range(4):
                        ii = tt.transpose(bkb[:, h, :], yab[:, h, :], ident_b[:])
                    return ii
                p.op("pe", tr6, reads=[r_yab, r_const], writes=[br])
                yield
                p.op("act", lambda a: a.copy(yaT[:, :, t * 128:(t + 1) * 128], bkb[:, 0:4, :]), reads=[br], writes=[r_yaT[t]])
                grel(bk)
                yield
        dbg_dump("ybT", ybT[:], r_ybT[t_lo:], [128, 4, LT])

        with ExitStack() as s7:
            wpa = sb(s7, "wpa", [128, 4, D], BF16)
            wpb = sb(s7, "wpb", [128, 4, D], BF16)
            wo = sb(s7, "wo", [128, 8, D], BF16)
            r_w7 = Res()
            G_bc = [sb(s7, "G_bc%d" % i, [128, D]) for i in range(2)]
            r_bc = Res()
            for i, r in enumerate((b, NB)):
                p.dma("sp", G_bc[i][:], mods_s[r:r + 1, 2 * D:3 * D].to_broadcast([128, D]), reads=[r_mods], writes=[r_bc])
            p.dma("pool", wpa[:], w_proj_a[l].rearrange("(k p) n -> p k n", p=128), writes=[r_w7])
            p.dma("pool", wpb[:], w_proj_b[l].rearrange("(k p) n -> p k n", p=128), writes=[r_w7])
            p.dma("pool", wo[:], w_out[l].rearrange("(k p) n -> p k n", p=128), writes=[r_w7])
            NGB = 4
            gab = [sb(s7, "gab%d" % i, [128, 2, 512], BF16) for i in range(NGB)]; r_gab = [Res() for _ in range(NGB)]
            y1_l = [sb(s7, "y1%d" % i, [128, 512]) for i in range(3)]; r_y1_l = [Res() for _ in range(3)]
            y2_l = [sb(s7, "y2%d" % i, [128, 512]) for i in range(3)]; r_y2_l = [Res() for _ in range(3)]
            yT_l = [sb(s7, "yT%d" % i, [128, 8, 512], BF16) for i in range(3)]; r_yT_l = [Res() for _ in range(3)]
            xr = [sb(s7, "xr%d" % i, [128, D]) for i in range(3)]; r_xr = [Res() for _ in range(3)]
            xo = [sb(s7, "xo%d" % i, [128, D]) for i in range(3)]; r_xo = [Res() for _ in range(3)]
            tok0 = t_lo * 128
            nx = 0
            groups7 = [(g0, min(512, LT - g0)) for g0 in range(tok0, LT, 512)]

            def m_fc(gi7, fc):
                g0, gw = groups7[gi7]
                yT, r_yT = yT_l[gi7 % 3], r_yT_l[gi7 % 3]
                idx = gi7 * 8 + fc
                G_ = gab[idx % NGB]; rG = r_gab[idx % NGB]
                y1, r_y1, y2, r_y2 = y1_l[idx % 3], r_y1_l[idx % 3], y2_l[idx % 3], r_y2_l[idx % 3]
                p.dma("sp", G_[:, 0, 0:gw], gT_s[fc * 128:(fc + 1) * 128, g0:g0 + gw], reads=[r_gT], writes=[rG])
                p.dma("sp", G_[:, 1, 0:gw], gT_s[1024 + fc * 128:1024 + (fc + 1) * 128, g0:g0 + gw], reads=[r_gT], writes=[rG])
                yield
                bA, rA = gps()
                bB_, rB_ = gps()

                def mm7a(tt):
                    for k in range(4):
                        tt.matmul(bA[:, 0:gw], wpa[:, k, fc * 128:(fc + 1) * 128], yaT[:, k, g0:g0 + gw], start=(k == 0), stop=(k == 3))
                    for k in range(4):
                        i = tt.matmul(bB_[:, 0:gw], wpb[:, k, fc * 128:(fc + 1) * 128], ybT[:, k, g0:g0 + gw], start=(k == 0), stop=(k == 3))
                    return i
                p.op("pe", mm7a, reads=[r_w7] + r_yaT[g0 // 128:(g0 + gw) // 128] + r_ybT[g0 // 128:(g0 + gw) // 128], writes=[rA, rB_])
                yield
                p.op("dve", lambda v: v.tensor_tensor(out=y1[:, 0:gw], in0=bA[:, 0:gw], in1=G_[:, 0, 0:gw], op=ALU.mult), reads=[rA, rG], writes=[r_y1])
                p.op("dve", lambda v: v.tensor_tensor(out=y2[:, 0:gw], in0=bB_[:, 0:gw], in1=G_[:, 1, 0:gw], op=ALU.mult), reads=[rB_, rG], writes=[r_y2])
                grel(bA, bB_)
                yield
                p.op("pool", lambda g: g.tensor_tensor(out=yT[:, fc, 0:gw], in0=y1[:, 0:gw], in1=y2[:, 0:gw], op=ALU.add),
                     reads=[r_y1, r_y2], writes=[r_yT])
                yield

            ntile7 = [0]

            def m_tile(gi7, tt_):
                g0, gw = groups7[gi7]
                yT, r_yT = yT_l[gi7 % 3], r_yT_l[gi7 % 3]
                t = g0 // 128 + tt_
                j = ntile7[0] % 3
                ntile7[0] += 1
                XR, rXR = xr[j], r_xr[j]
                XO, rXO = xo[j], r_xo[j]
                p.dma("sp", XR[:], tok_src(t), reads=src_res, writes=[rXR])
                gi = 1 if t < NTC else 0
                yield
                for half in range(2):
                    bk, br = gps()

                    def mm7b(tt, bk=bk, half=half):
                        for k in range(8):
                            i = tt.matmul(bk[:], yT[:, k, tt_ * 128:(tt_ + 1) * 128], wo[:, k, half * 512:(half + 1) * 512], start=(k == 0), stop=(k == 7))
                        return i
                    p.op("pe", mm7b, reads=[r_yT, r_w7], writes=[br])
                    yield
                    p.op("dve", lambda v, bk=bk, half=half: v.tensor_tensor(out=XO[:, half * 512:(half + 1) * 512], in0=bk[:],
                                                                          in1=G_bc[gi][:, half * 512:(half + 1) * 512], op=ALU.mult),
                         reads=[br, r_bc], writes=[rXO])
                    grel(bk)
                yield
                p.op("pool", lambda g: g.tensor_tensor(out=XO[:], in0=XO[:], in1=XR[:], op=ALU.add), reads=[rXO, rXR], writes=[rXO])
                yield
                if t < NTC:
                    p.dma("sp", ctx1_s[b, t * 128:(t + 1) * 128, :], XO[:], reads=[rXO], writes=[r_ctx1[b]])
                else:
                    p.dma("sp", x_dst[(t - NTC) * 128:(t - NTC + 1) * 128, :], XO[:], reads=[rXO], writes=dst_res, key=rXO)
                yield
            def p6_of(gi7):
                g0, gw = groups7[gi7]
                return [p6_tile(t) for t in range(g0 // 128, (g0 + gw) // 128)]
            ng7 = len(groups7)
            glist = p6_of(0) + [None]
            for gi7 in range(ng7 + 1):
                if gi7 < ng7:
                    glist += [m_fc(gi7, fc) for fc in range(8)]
                if gi7 >= 1:
                    glist += [m_tile(gi7 - 1, tt_) for tt_ in range(groups7[gi7 - 1][1] // 128)]
                if gi7 + 1 < ng7:
                    glist += p6_of(gi7 + 1)
                glist += [None]
            run_pipelined(glist, 3, stagger=1)
            p.barrier()
        dbg_dump("yaT", yaT[:], r_yaT[t_lo:], [128, 4, LT])
        s6.close()
        if dbg and (b, l) == dbg_bl:
            for nm, src, rr in (("x1", x1_s[b], r_x1[b]), ("ctx1", ctx1_s[b], r_ctx1[b])):
                if nm in dbg:
                    p.dma("sp", dbg_out[nm], src, reads=[rr])
            p.barrier()
        stY.close()
        stYb.close()

    dbg_bl = (0, 0)
    for l in range(DEPTH):
        layer_mods(l)
        for b in range(NB):
            block(b, l)
            if dbg:
                break
        if dbg:
            break
    p.barrier()
    top.close()
    return nc, p


def _rope_tables():
    inv = 10000.0 ** (-np.arange(0, 32, 2, dtype=np.float32) / 32.0)
    t = np.arange(L)
    pr = (t // 64).astype(np.float32)
    pc = (t % 64).astype(np.float32)
    ar = pr[:, None] * inv[None, :]
    ac = pc[:, None] * inv[None, :]
    tab = np.stack([np.cos(ar), np.cos(ac), np.sin(ar), np.sin(ac)], axis=1).astype(np.float32)
    return np.ascontiguousarray(tab.reshape(16, 128, 2, 2, 16).transpose(1, 0, 2, 3, 4))


def _gmask():
    base = np.zeros((2, 7, 128, 128), np.float32)
    pp = np.arange(128)[:, None]
    ff = np.arange(128)[None, :]
    for li in range(7):
        s_ = 1 << li
        base[0, li] = (((ff // s_) % 2 == 1) & ((pp // s_) == (ff // s_) - 1)).astype(np.float32)
        base[1, li] = (((pp // s_) % 2 == 1) & ((ff // s_) == (pp // s_) - 1)).astype(np.float32)
    m = np.zeros((128, 2, 7, 2, 128), np.float32)
    for d in range(2):
        for li in range(7):
            m[:, d, li, 0, :] = base[1 - d, li]
            m[:, d, li, 1, :] = base[d, li]
    return m


def make_in_maps(inputs, NB, ncores):
    maps = []
    rope = _rope_tables()
    gm = _gmask()
    for i in range(ncores):
        sl = slice(i * NB, (i + 1) * NB)
        m = {
            "x": np.ascontiguousarray(inputs["x"][sl]),
            "ctx": np.ascontiguousarray(inputs["ctx"][sl]),
            "c": np.ascontiguousarray(np.concatenate([inputs["c"][sl], inputs["c_ctx"][None, :]], axis=0)),
            "rope": rope,
            "gmask": gm,
        }
        for k in ("norm_w", "w_mod", "b_mod", "w_in", "conv_w", "gdn_norm_w", "q_norm_w", "k_norm_w", "sink",
                  "w_proj_a", "w_proj_b", "w_out"):
            m[k] = np.ascontiguousarray(inputs[k])
        m["a_log"] = np.ascontiguousarray(inputs["a_log"].reshape(DEPTH, 8))
        m["dt_bias"] = np.ascontiguousarray(inputs["dt_bias"].reshape(DEPTH, 8))
        maps.append(m)
    return maps


def kernel(**inputs):
    inputs = {k: np.asarray(v, dtype=np.float32) for k, v in inputs.items()}
    NB = 16 // NCORES
    nc, _ = build(NB)
    maps = make_in_maps(inputs, NB, NCORES)
    res = run_bass_kernel_spmd(nc, maps, core_ids=list(range(NCORES)))
    return np.concatenate([r["y"] for r in res.results], axis=0).astype(np.float32)
```

```python
import math
from contextlib import ExitStack
import numpy as np
import concourse.bass as bass
import concourse.mybir as mybir
from concourse.bass_utils import run_bass_kernel_spmd

F32 = mybir.dt.float32
F32R = mybir.dt.float32r
BF16 = mybir.dt.bfloat16
ALU = mybir.AluOpType
AF = mybir.ActivationFunctionType
AX = mybir.AxisListType

NCORES = 8
D = 1024
L = 2048
LC = 256
LT = L + LC
NT = LT // 128
NTC = LC // 128
DEPTH = 2
N_STATE = 1808
N_IN = 5392
EPS = 1e-6
BIG = 30000.0
GDN_PHASE = 1
O_QKV, O_BD, O_KVB, O_ZA, O_QB, O_ZB, O_GA, O_GB = 0, 1536, 1552, 1808, 2320, 2832, 3344, 4368
TP_BD, TP_KB, TP_VB, TP_ZA, TP_QB, TP_ZB, TP_W = 0, 16, 144, 272, 784, 1296, 1808


class Res:
    __slots__ = ("w", "r")

    def __init__(self):
        self.w = None
        self.r = []


class Prog:
    def __init__(self, nc):
        self.nc = nc
        self.eng = {"pe": nc.tensor, "act": nc.scalar, "dve": nc.vector, "pool": nc.gpsimd, "sp": nc.sync}
        self.sem = {k: nc.alloc_semaphore("s_" + k) for k in self.eng}
        self.cnt = {k: 0 for k in self.eng}
        self.waited = {}
        self.dma_sems = {}
        self.sem_cnt = {}
        self.free_sems = []
        self.keep = []
        self.pool_sems = []
        self.pool_done = 0
        self.ninst = 0

    def _wait(self, e, deps):
        best = {}
        for (s, v) in deps:
            k = id(s)
            if k not in best or best[k][1] < v:
                best[k] = (s, v)
        for k, (s, v) in best.items():
            if self.waited.get((e, k), 0) < v:
                self.eng[e].wait_ge(s, v)
                self.waited[(e, k)] = v

    @staticmethod
    def _deps(reads, writes):
        deps = []
        for r in reads:
            if r.w is not None:
                deps.append(r.w)
        for w in writes:
            if w.w is not None:
                deps.append(w.w)
            deps.extend(w.r)
        return deps

    @staticmethod
    def _mark(tok, reads, writes):
        for r in reads:
            r.r.append(tok)
            if len(r.r) > 24:
                best = {}
                for (s, v) in r.r:
                    if id(s) not in best or best[id(s)][1] < v:
                        best[id(s)] = (s, v)
                r.r = list(best.values())
        for w in writes:
            w.w = tok
            w.r = []

    def op(self, e, fn, reads=(), writes=()):
        self._wait(e, self._deps(reads, writes))
        inst = fn(self.eng[e])
        self.cnt[e] += 1
        self.ninst += 1
        inst.then_inc(self.sem[e], 1)
        tok = (self.sem[e], self.cnt[e])
        self._mark(tok, reads, writes)
        return tok

    def dma(self, e, out, in_, reads=(), writes=(), key=None):
        self._wait(e, self._deps(reads, writes))
        if e == "pool":
            sm = self.nc.alloc_semaphore("q%d" % len(self.pool_sems))
            self.pool_sems.append(sm)
            self.eng[e].dma_start(out=out, in_=in_).then_inc(sm, 16)
            tok = (sm, 16)
            self._mark(tok, reads, writes)
            return tok
        kres = key if key is not None else (writes[0] if writes else reads[0])
        k = id(kres)
        if k not in self.dma_sems:
            if self.free_sems:
                self.dma_sems[k] = self.free_sems.pop()
            else:
                sm = self.nc.alloc_semaphore("d%d" % len(self.sem_cnt))
                self.sem_cnt[id(sm)] = 0
                self.dma_sems[k] = sm
            self.keep.append(kres)
        s = self.dma_sems[k]
        self.sem_cnt[id(s)] += 16
        self.eng[e].dma_start(out=out, in_=in_).then_inc(s, 16)
        tok = (s, self.sem_cnt[id(s)])
        self._mark(tok, reads, writes)
        return tok

    def barrier(self):
        toks = [(self.sem[k], self.cnt[k]) for k in self.eng if self.cnt[k] > 0]
        toks += [(sm, self.sem_cnt[id(sm)]) for sm in self.dma_sems.values()]
        toks += [(sm, 16) for sm in self.pool_sems[self.pool_done:]]
        self.pool_done = len(self.pool_sems)
        for e in self.eng:
            self._wait(e, toks)
        self.free_sems.extend(self.dma_sems.values())
        self.dma_sems = {}
        self.keep = []


def build(NB=2, dbg=None):
    nc = bass.Bass("TRN2", target_bir_lowering=False)
    p = Prog(nc)
    R = NB + 1

    def din(name, shape, dt=F32):
        return nc.dram_tensor(name, list(shape), dt, kind="ExternalInput").ap()

    x_in = din("x", [NB, L, D])
    ctx_in = din("ctx", [NB, LC, D])
    c_in = din("c", [R, D])
    norm_w = din("norm_w", [DEPTH, D])
    w_mod = din("w_mod", [DEPTH, D, 3 * D])
    b_mod = din("b_mod", [DEPTH, 3 * D])
    w_in = din("w_in", [DEPTH, D, N_IN])
    conv_w = din("conv_w", [DEPTH, 5, 1536])
    a_log = din("a_log", [DEPTH, 8])
    dt_bias = din("dt_bias", [DEPTH, 8])
    gdn_norm_w = din("gdn_norm_w", [DEPTH, 128])
    q_norm_w = din("q_norm_w", [DEPTH, 64])
    k_norm_w = din("k_norm_w", [DEPTH, 64])
    sink = din("sink", [DEPTH, 8])
    w_proj_a = din("w_proj_a", [DEPTH, 512, D])
    w_proj_b = din("w_proj_b", [DEPTH, 512, D])
    w_out = din("w_out", [DEPTH, D, D])
    rope_in = din("rope", [128, 16, 2, 2, 16])
    gmask_in = din("gmask", [128, 2, 7, 2, 128])
    y_out = nc.dram_tensor("y", [NB, L, D], F32, kind="ExternalOutput").ap()

    def scratch(name, shape, dt=F32):
        return nc.dram_tensor(name, list(shape), dt, kind="Internal").ap()

    x1_s = scratch("x1_s", [NB, L, D])
    ctx1_s = scratch("ctx1_s", [NB, LC, D])
    mods_s = scratch("mods_s", [R, 3 * D])
    qkvT_s = scratch("qkvT_s", [1536, LT])
    gT_s = scratch("gT_s", [2048, LT], BF16)
    tokp_s = scratch("tokp_s", [LT, TP_W])
    ktok_s = scratch("ktok_s", [NT, 128, 4, 128], BF16)
    vtok_s = scratch("vtok_s", [NT, 128, 4, 128], BF16)
    o_s = scratch("o_s", [2, LT, 512])
    yb_s = scratch("yb_s", [LT, 512], BF16)
    r_yb_s = [Res() for _ in range(NT)]
    r_ktok_s = Res()
    r_vtok_s = Res()
    r_o_s = [[Res() for _ in range(NT)] for _ in range(2)]
    r_x1 = [Res() for _ in range(NB)]
    r_ctx1 = [Res() for _ in range(NB)]
    r_mods = Res()
    r_qkvT = Res()
    r_gT = Res()
    r_tokp = Res()
    dbg_out = {}
    if dbg:
        for name, shape in dbg.items():
            dbg_out[name] = nc.dram_tensor("dbg_" + name, list(shape), F32, kind="ExternalOutput").ap()

    top = ExitStack()

    uid = [0]

    def sb(stack, name, shape, dt=F32):
        uid[0] += 1
        return stack.enter_context(nc.sbuf_tensor("%s_%d" % (name, uid[0]), list(shape), dt))

    psum_all = nc.alloc_psum_tensor("psum_all", [128, 8, 512], F32)
    banks = [psum_all[:, i, :] for i in range(8)]
    bank_r = [Res() for _ in range(8)]
    bank_i = [0]
    nrot = [8]

    def ps():
        i = bank_i[0] % nrot[0]
        bank_i[0] += 1
        return banks[i], bank_r[i]

    def ps2():
        while (bank_i[0] % nrot[0]) % 2 != 0:
            bank_i[0] += 1
        i = bank_i[0] % nrot[0]
        bank_i[0] += 2
        return (banks[i], banks[i + 1]), (bank_r[i], bank_r[i + 1])

    free_banks = list(range(8))

    def gps():
        for i in free_banks:
            if (i ^ 1) not in free_banks:
                free_banks.remove(i)
                return banks[i], bank_r[i]
        i = free_banks.pop(0)
        return banks[i], bank_r[i]

    def gps2():
        for i in (0, 2, 4, 6):
            if i in free_banks and (i + 1) in free_banks:
                free_banks.remove(i)
                free_banks.remove(i + 1)
                return psum_all[:, i:i + 2, :], (bank_r[i], bank_r[i + 1]), i
        raise AssertionError("no aligned PSUM bank pair free")

    def grel2(i):
        assert i not in free_banks and (i + 1) not in free_banks
        free_banks.extend([i, i + 1])

    def grel(*bks):
        for bk_ in bks:
            for i in range(8):
                if banks[i] is bk_:
                    assert i not in free_banks
                    free_banks.append(i)

    def run_pipelined(gen_list, depth, stagger=0):
        pending = list(gen_list)
        live = []
        since = stagger
        while pending or live:
            if pending and pending[0] is None:
                if not live:
                    pending.pop(0)
                    since = stagger
            elif pending and len(live) < depth and since >= stagger:
                live.append(pending.pop(0))
                since = 0
            since += 1
            for g_ in list(live):
                try:
                    next(g_)
                except StopIteration:
                    live.remove(g_)

    ident_f = sb(top, "ident_f", [128, 128]); r_const = Res()
    ident_b = sb(top, "ident_b", [128, 128], BF16)
    ones_f = sb(top, "ones_f", [128, 128])
    ones_b = sb(top, "ones_b", [128, 128], BF16)
    m_incl = sb(top, "m_incl", [128, 2, 128])
    m_strict = sb(top, "m_strict", [128, 2, 128])
    tri_f = sb(top, "tri_f", [128, 2, 128])
    mk_prev = sb(top, "mk_prev", [128, 128], BF16)
    mk_next = sb(top, "mk_next", [128, 128], BF16)
    rope_t = sb(top, "rope_t", [128, 16, 2, 2, 16])
    gmask = sb(top, "gmask_sb", [128, 2, 7, 2, 128], BF16)
    scT = sb(top, "scT", [128, 8, R])

    def cst(fn):
        p.op("pool", fn, reads=[r_const], writes=[r_const])

    def asel(out_ap, op_, fill, pat, cm):
        cst(lambda g: g.affine_select(out=out_ap, in_=out_ap, compare_op=op_, fill=fill, base=0, pattern=pat, channel_multiplier=cm))
    cst(lambda g: g.memset(ident_f[:], 0.0))
    asel(ident_f[:], ALU.not_equal, 1.0, [[-1, 128]], 1)
    cst(lambda g: g.memset(ones_f[:], 1.0))
    cst(lambda g: g.memset(ones_b[:], 1.0))
    cst(lambda g: g.memset(m_incl[:], 0.0))
    asel(m_incl[:, 0, :], ALU.is_ge, BIG, [[-1, 128]], 1)
    asel(m_incl[:, 1, :], ALU.is_ge, BIG, [[1, 128]], -1)
    cst(lambda g: g.memset(m_strict[:], -1.0))
    asel(m_strict[:, 0, :], ALU.is_gt, 0.0, [[-1, 128]], 1)
    asel(m_strict[:, 1, :], ALU.is_gt, 0.0, [[1, 128]], -1)
    cst(lambda g: g.memset(tri_f[:], 1.0))
    asel(tri_f[:, 0, :], ALU.is_ge, 0.0, [[1, 128]], -1)
    asel(tri_f[:, 1, :], ALU.is_ge, 0.0, [[-1, 128]], 1)
    cst(lambda g: g.memset(mk_prev[:], 1.0))
    asel(mk_prev[:], ALU.is_ge, 0.0, [[-1, 128]], 1)
    cst(lambda g: g.memset(mk_next[:], 1.0))
    asel(mk_next[:], ALU.is_ge, 0.0, [[1, 128]], -1)
    p.op("dve", lambda v: v.tensor_copy(ident_b[:], ident_f[:]), reads=[r_const], writes=[r_const])
    p.dma("sp", rope_t[:], rope_in, writes=[r_const])
    p.dma("pool", gmask[:], gmask_in, writes=[r_const])

    with ExitStack() as st:
        crow = sb(st, "crow", [R, D]); r_crow = Res()
        p.dma("sp", crow[:], c_in, writes=[r_crow])
        p.op("act", lambda a: a.activation(out=crow[:], in_=crow[:], func=AF.Silu), reads=[r_crow], writes=[r_crow])
        bk, br = ps()

        def tr(t):
            for k in range(8):
                i = t.transpose(bk[:, k * R:(k + 1) * R], crow[0:R, k * 128:(k + 1) * 128], ident_f[0:R, 0:R])
            return i
        p.op("pe", tr, reads=[r_crow, r_const], writes=[br])
        p.op("dve", lambda v: v.tensor_copy(scT[:].rearrange("p k r -> p (k r)"), bk[:, 0:8 * R]), reads=[br], writes=[r_const])
        p.barrier()


    def layer_mods(l):
        with ExitStack() as st:
            wm = [sb(st, "wm%d" % i, [128, 8, 512]) for i in range(2)]
            r_wm = [Res(), Res()]
            bm = sb(st, "bm", [R, 3 * D]); r_bm = Res()
            rows = sb(st, "rows", [R, 3 * D]); r_rows = Res()
            p.dma("sp", bm[:], b_mod[l:l + 1, :].to_broadcast([R, 3 * D]), writes=[r_bm])

            def ld(j):
                p.dma("sp", wm[j % 2][:], w_mod[l, :, j * 512:(j + 1) * 512].rearrange("(k p) n -> p k n", p=128),
                      writes=[r_wm[j % 2]])
            ld(0)
            for j in range(6):
                if j + 1 < 6:
                    ld(j + 1)
                bk, br = ps()

                def mm(t, j=j, bk=bk):
                    for k in range(8):
                        i = t.matmul(bk[0:R, :], scT[:, k, :], wm[j % 2][:, k, :], start=(k == 0), stop=(k == 7))
                    return i
                p.op("pe", mm, reads=[r_wm[j % 2], r_const], writes=[br])
                p.op("dve", lambda v, j=j, bk=bk: v.tensor_tensor(out=rows[:, j * 512:(j + 1) * 512], in0=bk[0:R, :],
                                                                   in1=bm[:, j * 512:(j + 1) * 512], op=ALU.add),
                     reads=[br, r_bm], writes=[r_rows])
            p.dma("sp", mods_s, rows[:], reads=[r_rows], writes=[r_mods])
            p.barrier()

    def block(b, l):
        x_src = x_in[b] if l == 0 else x1_s[b]
        c_src = ctx_in[b] if l == 0 else ctx1_s[b]
        x_dst = x1_s[b] if l == 0 else y_out[b]
        src_res = [] if l == 0 else [r_x1[b], r_ctx1[b]]
        dst_res = [r_x1[b]] if l == 0 else []
        upd_ctx = (l == 0)
        t_lo = 0 if upd_ctx else NTC

        def tok_src(t):
            return c_src[t * 128:(t + 1) * 128, :] if t < NTC else x_src[(t - NTC) * 128:(t - NTC + 1) * 128, :]

        with ExitStack() as st:
            hT = sb(st, "hT", [128, 8, LT], BF16)
            r_hT = [Res() for _ in range(NT)]
            wt = [sb(st, "wt%d" % i, [128, 8, 512], BF16) for i in range(2)]
            r_wt = [Res(), Res()]
            wtiles = [(0, 512, "qkv", 0), (512, 512, "qkv", 512), (1024, 512, "qkv", 1024),
                      (O_BD, 272, "tm", TP_BD), (O_ZA, 512, "tmz", TP_ZA), (O_QB, 512, "tm", TP_QB),
                      (O_ZB, 512, "tmz", TP_ZB)]
            wtiles += [(O_GA + i * 512, 512, "g", i * 512) for i in range(4)]

            def ldw(j):
                c0, w, _, _ = wtiles[j]
                p.dma("pool", wt[j % 2][:, :, 0:w], w_in[l, :, c0:c0 + w].rearrange("(k p) n -> p k n", p=128),
                      writes=[r_wt[j % 2]])
            ldw(0)
            with ExitStack() as st1:
                A_bc = [sb(st1, "A_bc%d" % i, [128, D]) for i in range(2)]
                S_bc = [sb(st1, "S_bc%d" % i, [128, D]) for i in range(2)]
                r_bc = Res()
                nw = sb(st1, "nw_bc", [128, D]); r_nw = Res()
                tmp_l = [sb(st1, "sc_bc%d" % i, [128, D]) for i in range(2)]; r_tmp_l = [Res(), Res()]
                p.dma("act", nw[:], norm_w[l:l + 1, :].to_broadcast([128, D]), writes=[r_nw])
                for i, r in enumerate((b, NB)):
                    p.dma("act", S_bc[i][:], mods_s[r:r + 1, 0:D].to_broadcast([128, D]), reads=[r_mods], writes=[r_bc], key=r_tmp_l[i])
                    p.dma("act", tmp_l[i][:], mods_s[r:r + 1, D:2 * D].to_broadcast([128, D]), reads=[r_mods], writes=[r_tmp_l[i]])
                    p.op("dve", lambda v, i=i: v.scalar_tensor_tensor(out=A_bc[i][:], in0=tmp_l[i][:], scalar=1.0, in1=nw[:],
                                                                       op0=ALU.add, op1=ALU.mult),
                         reads=[r_tmp_l[i], r_nw], writes=[r_bc])
                NXB = 4
                xt = [sb(st1, "xt%d" % i, [128, D]) for i in range(NXB)]
                r_xt = [Res() for _ in range(NXB)]
                r_xt2 = [Res() for _ in range(NXB)]
                junk_l = [sb(st1, "junk%d" % i, [128, D]) for i in range(3)]; r_junk_l = [Res() for _ in range(3)]
                t32_l = [sb(st1, "t32%d" % i, [128, D]) for i in range(3)]; r_t32_l = [Res() for _ in range(3)]
                hb_l = [sb(st1, "hb%d" % i, [128, D], BF16) for i in range(3)]; r_hb_l = [Res() for _ in range(3)]
                ss_l = [sb(st1, "ss%d" % i, [128, 2]) for i in range(3)]; r_ss_l = [Res() for _ in range(3)]

                D1 = 3

                def p1_tile(t):
                    i = 1 if t < NTC else 0
                    X, rX_ = xt[t % NXB], r_xt[t % NXB]
                    j = t % D1
                    junk, r_junk, t32, r_t32 = junk_l[j], r_junk_l[j], t32_l[j], r_t32_l[j]
                    hb, r_hb, ss, r_ss = hb_l[j], r_hb_l[j], ss_l[j], r_ss_l[j]
                    p.dma("sp", X[0:64, :], tok_src(t)[0:64, :], reads=src_res, writes=[rX_])
                    p.dma("act", X[64:128, :], tok_src(t)[64:128, :], reads=src_res, writes=[rX_])
                    yield
                    p.op("act", lambda a: a.activation(out=junk[:], in_=X[:], func=AF.Square, accum_out=ss[:, 0:1]),
                         reads=[rX_], writes=[r_junk, r_ss])
                    yield
                    p.op("dve", lambda v: v.tensor_scalar(out=ss[:, 1:2], in0=ss[:, 0:1], scalar1=1.0 / D, scalar2=EPS,
                                                          op0=ALU.mult, op1=ALU.add), reads=[r_ss], writes=[r_ss])
                    yield
                    p.op("act", lambda a: a.activation(out=ss[:, 1:2], in_=ss[:, 1:2], func=AF.Sqrt), reads=[r_ss], writes=[r_ss])
                    yield
                    p.op("dve", lambda v: v.reciprocal(ss[:, 1:2], ss[:, 1:2]), reads=[r_ss], writes=[r_ss])
                    p.op("dve", lambda v: v.scalar_tensor_tensor(out=t32[:], in0=X[:], scalar=ss[:, 1:2], in1=A_bc[i][:],
                                                                 op0=ALU.mult, op1=ALU.mult),
                         reads=[rX_, r_ss, r_bc], writes=[r_t32])
                    yield
                    p.op("pool", lambda g: g.tensor_tensor(out=hb[:], in0=t32[:], in1=S_bc[i][:], op=ALU.add),
                         reads=[r_t32, r_bc], writes=[r_hb])
                    yield
                    bk, br = gps()
                    bkb = bk[:].bitcast(BF16)

                    def tr(tt):
                        for k in range(8):
                            ii = tt.transpose(bkb[:, k * 128:(k + 1) * 128], hb[:, k * 128:(k + 1) * 128], ident_b[:])
                        return ii
                    p.op("pe", tr, reads=[r_hb, r_const], writes=[br])
                    yield
                    p.op("act", lambda a: a.copy(hT[:, :, t * 128:(t + 1) * 128], bkb.rearrange("p (k n) -> p k n", k=8)),
                         reads=[br], writes=[r_hT[t]])
                    grel(bk)
                    yield
                run_pipelined([p1_tile(t) for t in range(NT)], D1, stagger=2)
                p.barrier()
            if dbg and "hT" in dbg and (b, l) == dbg_bl:
                with ExitStack() as sd:
                    d32 = sb(sd, "d32", [128, 8, LT]); r_d = Res()
                    p.op("dve", lambda v: v.tensor_copy(d32[:], hT[:]), reads=r_hT, writes=[r_d])
                    tk = p.dma("sp", dbg_out["hT"], d32[:], reads=[r_d])
                    p.barrier()

            with ExitStack() as st2:
                stg = [sb(st2, "stg%d" % i, [128, LT]) for i in range(2)]
                r_stg = [Res(), Res()]
                stgb = [sb(st2, "stgb%d" % i, [128, LT], BF16) for i in range(2)]
                r_stgb = [Res(), Res()]
                stm = [sb(st2, "stm%d" % i, [128, 512]) for i in range(2)]
                r_stm = [Res(), Res()]
                nfm = 0
                ntm = 0
                for j, (c0, w, kind, doff) in enumerate(wtiles):
                    if j + 1 < len(wtiles):
                        ldw(j + 1)
                    W = wt[j % 2]
                    rW = r_wt[j % 2]
                    if kind in ("qkv", "g"):
                        tok0 = 0 if (kind == "qkv" or upd_ctx) else LC
                        groups = []
                        a0 = tok0
                        while a0 < LT:
                            groups.append((a0, min(512, LT - a0)))
                            a0 += 512
                        for c in range(w // 128):
                            if kind == "qkv":
                                S_, rS = stg[nfm % 2], r_stg[nfm % 2]
                            else:
                                S_, rS = stgb[nfm % 2], r_stgb[nfm % 2]
                            nfm += 1
                            for (g0, gw) in groups:
                                bk, br = ps()

                                def mm(t, bk=bk, W=W, c=c, g0=g0, gw=gw):
                                    for k in range(8):
                                        i = t.matmul(bk[:, 0:gw], W[:, k, c * 128:(c + 1) * 128], hT[:, k, g0:g0 + gw],
                                                     start=(k == 0), stop=(k == 7))
                                    return i
                                p.op("pe", mm, reads=[rW] + r_hT[g0 // 128:(g0 + gw) // 128], writes=[br])
                                if kind == "qkv":
                                    p.op("dve", lambda v, bk=bk, S_=S_, g0=g0, gw=gw: v.tensor_copy(S_[:, g0:g0 + gw], bk[:, 0:gw]),
                                         reads=[br], writes=[rS])
                                else:
                                    p.op("act", lambda a, bk=bk, S_=S_, g0=g0, gw=gw: a.activation(out=S_[:, g0:g0 + gw], in_=bk[:, 0:gw],
                                                                                                   func=AF.Sigmoid),
                                         reads=[br], writes=[rS])
                            row0 = doff + c * 128
                            if kind == "qkv":
                                p.dma("sp", qkvT_s[row0:row0 + 128, :], S_[:], reads=[rS], writes=[r_qkvT])
                            else:
                                p.dma("sp", gT_s[row0:row0 + 128, tok0:LT], S_[:, tok0:LT], reads=[rS], writes=[r_gT])
                    else:
                        t0 = 0 if (kind == "tm" and c0 == O_BD) else t_lo
                        for t in range(t0, NT):
                            bk, br = ps()
                            S_, rS = stm[ntm % 2], r_stm[ntm % 2]
                            ntm += 1

                            def mm(tt, bk=bk, W=W, t=t, w=w):
                                for k in range(8):
                                    i = tt.matmul(bk[:, 0:w], hT[:, k, t * 128:(t + 1) * 128], W[:, k, 0:w],
                                                  start=(k == 0), stop=(k == 7))
                                return i
                            p.op("pe", mm, reads=[rW, r_hT[t]], writes=[br])
                            if kind == "tmz":
                                p.op("act", lambda a, bk=bk, S_=S_, w=w: a.activation(out=S_[:, 0:w], in_=bk[:, 0:w], func=AF.Silu),
                                     reads=[br], writes=[rS])
                            else:
                                p.op("dve", lambda v, bk=bk, S_=S_, w=w: v.tensor_copy(S_[:, 0:w], bk[:, 0:w]), reads=[br], writes=[rS])
                            p.dma("sp", tokp_s[t * 128:(t + 1) * 128, doff:doff + w], S_[:, 0:w], reads=[rS], writes=[r_tokp])
                p.barrier()
        if dbg and (b, l) == dbg_bl:
            for nm, src, rr in (("qkvT", qkvT_s, r_qkvT), ("tokp", tokp_s, r_tokp)):
                if nm in dbg:
                    p.dma("sp", dbg_out[nm], src, reads=[rr])
            p.barrier()

        def dbg_dump(name, tile_ap, reads, shape, direct=False):
            if dbg and name in dbg and (b, l) == dbg_bl and direct:
                p.dma("sp", dbg_out[name], tile_ap, reads=reads)
                p.barrier()
            elif dbg and name in dbg and (b, l) == dbg_bl:
                with ExitStack() as sd:
                    d32 = sb(sd, "d32_" + name, shape); r_d = Res()
                    p.op("dve", lambda v: v.tensor_copy(d32[:], tile_ap), reads=reads, writes=[r_d])
                    p.dma("sp", dbg_out[name], d32[:], reads=[r_d])
                    p.barrier()

        stYb = ExitStack()
        ybT = sb(stYb, "ybT", [128, 4, LT], BF16); r_ybT = [Res() for _ in range(NT)]

        with ExitStack() as stG:
            qT = sb(stG, "qT", [128, 4, LT], BF16)
            kT = sb(stG, "kT", [128, 4, LT], BF16)
            r_qT = [Res() for _ in range(4)]
            r_kT = [Res() for _ in range(4)]
            beta = sb(stG, "beta", [128, NT, 8])
            gcs = sb(stG, "gcs", [128, NT, 8])
            egc = sb(stG, "egc", [128, NT, 8])
            bge = sb(stG, "bge", [128, NT, 8])
            ekd = sb(stG, "ekd", [128, NT, 8])
            egl = sb(stG, "egl", [128, NT, 8])
            r_gate = Res()
            sGt = ExitStack()
            bdt = sb(sGt, "bdt", [128, NT, 16]); r_bdt = Res()
            ab = sb(sGt, "ab", [128, 16]); r_ab = Res()
            gg = sb(sGt, "gg", [128, NT, 8]); r_gg = Res()
            gg2 = sb(sGt, "gg2", [128, 2, NT, 4])
            gl = sb(sGt, "gl", [128, NT, 8]); r_gl = Res()

            def gates_gen():
                p.dma("sp", bdt[:], tokp_s[:, 0:16].rearrange("(t p) c -> p t c", p=128), reads=[r_tokp], writes=[r_bdt])
                p.dma("sp", ab[:, 0:8], a_log[l:l + 1, :].to_broadcast([128, 8]), writes=[r_ab])
                p.dma("sp", ab[:, 8:16], dt_bias[l:l + 1, :].to_broadcast([128, 8]), writes=[r_ab])
                yield
                p.op("act", lambda a: a.activation(out=beta[:], in_=bdt[:, :, 0:8], func=AF.Sigmoid), reads=[r_bdt], writes=[r_gate])
                p.op("dve", lambda v: v.tensor_tensor(out=gg[:], in0=bdt[:, :, 8:16], in1=ab[:, 8:16].unsqueeze(1).to_broadcast([128, NT, 8]),
                                                      op=ALU.add), reads=[r_bdt, r_ab], writes=[r_gg])
                yield
                p.op("act", lambda a: a.activation(out=gg[:], in_=gg[:], func=AF.Exp), reads=[r_gg], writes=[r_gg])
                yield
                p.op("dve", lambda v: v.tensor_scalar_add(gg[:], gg[:], 1.0), reads=[r_gg], writes=[r_gg])
                yield
                p.op("act", lambda a: a.activation(out=gg[:], in_=gg[:], func=AF.Ln), reads=[r_gg], writes=[r_gg])
                p.op("act", lambda a: a.activation(out=ab[:, 0:8], in_=ab[:, 0:8], func=AF.Exp), reads=[r_ab], writes=[r_ab])
                yield
                for d_ in range(2):
                    p.op("dve", lambda v, d_=d_: v.scalar_tensor_tensor(out=gg2[:, d_], in0=gg[:, :, d_ * 4:(d_ + 1) * 4], scalar=-1.0,
                                                                        in1=ab[:, d_ * 4:(d_ + 1) * 4].unsqueeze(1).to_broadcast([128, NT, 4]),
                                                                        op0=ALU.mult, op1=ALU.mult), reads=[r_gg, r_ab], writes=[r_gg])
                yield
                bk, br = gps()
                bk2, br2 = gps()
                NH_ = NT * 4

                def mmg(t):
                    t.matmul(bk[:, 0:NH_], tri_f[:, 0, :], gg2[:, 0].rearrange("p t h -> p (t h)"), start=True, stop=True)
                    t.matmul(bk[:, NH_:2 * NH_], tri_f[:, 1, :], gg2[:, 1].rearrange("p t h -> p (t h)"), start=True, stop=True)
                    return t.matmul(bk2[:, 0:2 * NH_], ones_f[:], gg2[:].rearrange("p d t h -> p (d t h)"), start=True, stop=True)
                p.op("pe", mmg, reads=[r_gg, r_const], writes=[br, br2])
                yield
                p.op("dve", lambda v: v.tensor_copy(gcs[:].rearrange("p t (d h) -> p d t h", d=2), bk[:, 0:2 * NH_].rearrange("p (d t h) -> p d t h", d=2, h=4)),
                     reads=[br], writes=[r_gate])
                p.op("dve", lambda v: v.tensor_copy(gl[:].rearrange("p t (d h) -> p d t h", d=2), bk2[:, 0:2 * NH_].rearrange("p (d t h) -> p d t h", d=2, h=4)),
                     reads=[br2], writes=[r_gl])
                grel(bk, bk2)
                yield
                p.op("act", lambda a: a.activation(out=egc[:], in_=gcs[:], func=AF.Exp), reads=[r_gate], writes=[r_gate])
                p.op("act", lambda a: a.activation(out=egl[:], in_=gl[:], func=AF.Exp), reads=[r_gl], writes=[r_gate])
                yield
                p.op("dve", lambda v: v.tensor_tensor(out=gl[:], in0=gl[:], in1=gcs[:], op=ALU.subtract), reads=[r_gl, r_gate], writes=[r_gl])
                yield
                p.op("act", lambda a: a.activation(out=ekd[:], in_=gl[:], func=AF.Exp), reads=[r_gl], writes=[r_gate])
                p.op("dve", lambda v: v.tensor_tensor(out=bge[:], in0=beta[:], in1=egc[:], op=ALU.mult), reads=[r_gate], writes=[r_gate])
                yield
            with ExitStack() as s3:
                NP = LT + 8
                NA = LT + 4
                pad = [sb(s3, "pad%d" % i, [128, NP]) for i in range(3)]
                r_pad = [Res(), Res(), Res()]
                dgw_l = [sb(s3, "dgw%d" % i, [128, 5, 128]) for i in range(3)]; r_dgw_l = [Res(), Res(), Res()]
                sl_l = [sb(s3, "sl%d" % i, [128, NA]) for i in range(3)]; r_sl_l = [Res(), Res(), Res()]
                sq_l = [sb(s3, "sq%d" % i, [128, NA], BF16) for i in range(3)]; r_sq_l = [Res(), Res(), Res()]
                rinv_l = [sb(s3, "rinv%d" % i, [128, NA]) for i in range(3)]; r_rinv_l = [Res(), Res(), Res()]
                vTb_l = [sb(s3, "vTb%d" % i, [128, LT], BF16) for i in range(3)]; r_vTb_l = [Res(), Res(), Res()]
                cwrow = sb(s3, "cwrow", [5, 1536]); r_cwrow = Res()
                cw = sb(s3, "cw", [128, 12, 5]); r_cw = Res()
                stk = [sb(s3, "stk%d" % i, [128, 8, 128], BF16) for i in range(3)]; r_stk = [Res() for _ in range(3)]
                nstk = [0]
                p.op("pool", lambda g: (g.memset(pad[0][:], 0.0), g.memset(pad[1][:], 0.0), g.memset(pad[2][:], 0.0))[2], writes=r_pad)
                p.dma("sp", cwrow[:], conv_w[l], writes=[r_cwrow])
                bk, br = ps()

                def trc(t, bk=bk):
                    for c in range(12):
                        i = t.transpose(bk[:, c * 5:(c + 1) * 5], cwrow[0:5, c * 128:(c + 1) * 128], ident_f[0:5, 0:5])
                    return i
                p.op("pe", trc, reads=[r_cwrow, r_const], writes=[br])
                p.op("dve", lambda v, bk=bk: v.tensor_copy(cw[:].rearrange("p c k -> p (c k)"), bk[:, 0:60]), reads=[br], writes=[r_cw])

                def p3_chunk(ch):
                    kind, h = ch // 4, ch % 4
                    i2 = ch % 3
                    sl, r_sl, sq, r_sq, rinv, r_rinv = sl_l[i2], r_sl_l[i2], sq_l[i2], r_sq_l[i2], rinv_l[i2], r_rinv_l[i2]
                    P_, rP = pad[i2], r_pad[i2]
                    DG, rDG = dgw_l[i2], r_dgw_l[i2]
                    VT, rVT = vTb_l[i2], r_vTb_l[i2]
                    p.dma("sp", P_[:, 2:2 + LC], qkvT_s[ch * 128:(ch + 1) * 128, 0:LC], reads=[r_qkvT], writes=[rP])
                    p.dma("sp", P_[:, 6 + LC:6 + LT], qkvT_s[ch * 128:(ch + 1) * 128, LC:LT], reads=[r_qkvT], writes=[rP])
                    p.op("dve", lambda v: v.tensor_tensor(out=DG[:].bitcast(F32R), in0=ident_f[:].unsqueeze(1).to_broadcast([128, 5, 128]),
                                                          in1=cw[:, ch, :].unsqueeze(2).to_broadcast([128, 5, 128]), op=ALU.mult),
                         reads=[r_const, r_cw], writes=[rDG])
                    yield
                    a0 = 0
                    while a0 < NA:
                        gw = min(512, NA - a0)
                        bk, br = gps()

                        def mmc(tt, bk=bk, a0=a0, gw=gw):
                            for k in range(5):
                                i = tt.matmul(bk[:, 0:gw], DG[:, k, :].bitcast(F32R), P_[:, a0 + k:a0 + k + gw].bitcast(F32R),
                                              start=(k == 0), stop=(k == 4))
                            return i
                        p.op("pe", mmc, reads=[rP, rDG], writes=[br])
                        yield
                        p.op("act", lambda a, bk=bk, a0=a0, gw=gw: a.activation(out=sl[:, a0:a0 + gw], in_=bk[:, 0:gw], func=AF.Silu),
                             reads=[br], writes=[r_sl])
                        grel(bk)
                        a0 += 512
                    yield
                    if kind < 2:
                        p.op("dve", lambda g: g.tensor_tensor(out=sq[:], in0=sl[:], in1=sl[:], op=ALU.mult), reads=[r_sl], writes=[r_sq])
                        yield
                        a0 = 0
                        while a0 < NA:
                            gw = min(512, NA - a0)
                            bk, br = gps()
                            p.op("pe", lambda t, bk=bk, a0=a0, gw=gw: t.matmul(bk[:, 0:gw], ones_b[:], sq[:, a0:a0 + gw], start=True, stop=True),
                                 reads=[r_sq, r_const], writes=[br])
                            yield
                            p.op("dve", lambda v, bk=bk, a0=a0, gw=gw: v.tensor_scalar_add(rinv[:, a0:a0 + gw], bk[:, 0:gw], EPS),
                                 reads=[br], writes=[r_rinv])
                            grel(bk)
                            a0 += 512
                        yield
                        p.op("act", lambda a: a.activation(out=rinv[:], in_=rinv[:], func=AF.Ln), reads=[r_rinv], writes=[r_rinv])
                        p.op("act", lambda a: a.activation(out=rinv[:], in_=rinv[:], func=AF.Exp, scale=-0.5), reads=[r_rinv], writes=[r_rinv])
                        yield
                        dst, rdst = (qT, r_qT[h]) if kind == 0 else (kT, r_kT[h])
                        scl = (128.0 ** -0.5) if kind == 0 else 1.0
                        for (d0, d1, s0) in ((0, LC, 0), (LC, LT, LC + 4)):
                            p.op("dve", lambda v, d0=d0, d1=d1, s0=s0: v.scalar_tensor_tensor(
                                out=dst[:, h, d0:d1], in0=sl[:, s0:s0 + d1 - d0], scalar=scl, in1=rinv[:, s0:s0 + d1 - d0],
                                op0=ALU.mult, op1=ALU.mult), reads=[r_sl, r_rinv], writes=[rdst])
                        if kind == 1:
                            srcT, rsrc, dtok, rdt = kT[:, h, :], r_kT[h], ktok_s, r_ktok_s
                    else:
                        p.op("act", lambda a: a.copy(VT[:, 0:LC], sl[:, 0:LC]), reads=[r_sl], writes=[rVT])
                        p.op("act", lambda a: a.copy(VT[:, LC:LT], sl[:, LC + 4:LT + 4]), reads=[r_sl], writes=[rVT])
                        srcT, rsrc, dtok, rdt = VT[:, :], rVT, vtok_s, r_vtok_s
                    yield
                    if kind >= 1:
                        for t0 in range(0, NT, 8):
                            n = min(8, NT - t0)
                            bk, br = gps()
                            bkb = bk[:].bitcast(BF16)

                            def trk(tt, bkb=bkb, t0=t0, n=n):
                                for i in range(n):
                                    ii = tt.transpose(bkb[:, i * 128:(i + 1) * 128], srcT[:, (t0 + i) * 128:(t0 + i + 1) * 128], ident_b[:])
                                return ii
                            p.op("pe", trk, reads=[rsrc, r_const], writes=[br])
                            yield
                            SK, rSK = stk[nstk[0] % 3], r_stk[nstk[0] % 3]
                            nstk[0] += 1
                            p.op("act", lambda a, bkb=bkb, n=n, SK=SK: a.copy(SK[:, 0:n, :], bkb[:, 0:n * 128].rearrange("p (t d) -> p t d", d=128)),
                                 reads=[br], writes=[rSK])
                            grel(bk)
                            p.dma("sp", dtok[t0:t0 + n, :, h, :].rearrange("t p d -> p t d"), SK[:, 0:n, :], reads=[rSK], writes=[rdt], key=rSK)
                            yield
                run_pipelined([gates_gen()] + [p3_chunk(ch) for ch in range(12)], 3, stagger=8)
                p.barrier()
            sGt.close()
            dbg_dump("qT", qT[:], r_qT, [128, 4, LT])
            dbg_dump("kT", kT[:], r_kT, [128, 4, LT])
            if dbg and "gates" in dbg and (b, l) == dbg_bl:
                with ExitStack() as sd:
                    d32 = sb(sd, "d32g", [128, NT, 4, 8]); r_d = Res()
                    for i, tl in enumerate((beta, gcs, egc, ekd)):
                        p.op("dve", lambda v, i=i, tl=tl: v.tensor_copy(d32[:, :, i, :], tl[:]), reads=[r_gate], writes=[r_d])
                    p.dma("sp", dbg_out["gates"], d32[:], reads=[r_d])
                    p.barrier()

            sAtt = ExitStack()
            KT = sb(sAtt, "KT", [64, 2, LT], BF16); r_KT = [Res() for _ in range(NT)]
            V1 = sb(sAtt, "V1", [128, NT, 2, 65], BF16); r_V1 = [Res() for _ in range(NT)]
            QTt = [sb(sAtt, "QTt%d" % i, [64, 8, 128], BF16) for i in range(3)]; r_QTt = [Res() for _ in range(3)]
            nwqk = sb(sAtt, "nwqk", [128, 10, 64]); r_nw2 = Res()
            esink = sb(sAtt, "esink", [128, 8]); r_esink = Res()
            p.dma("sp", nwqk[:, 0:8, :], q_norm_w[l:l + 1, :].unsqueeze(1).to_broadcast([128, 8, 64]), writes=[r_nw2])
            p.dma("sp", nwqk[:, 8:10, :], k_norm_w[l:l + 1, :].unsqueeze(1).to_broadcast([128, 2, 64]), writes=[r_nw2])
            p.dma("sp", esink[:], sink[l:l + 1, :].to_broadcast([128, 8]), writes=[r_esink])
            p.op("act", lambda a: a.activation(out=esink[:], in_=esink[:], func=AF.Exp), reads=[r_esink], writes=[r_esink])
            p.op("pool", lambda g: g.memset(V1[:], 1.0), writes=r_V1)
            qk = [sb(sAtt, "qk%d" % i, [128, 10, 64]) for i in range(2)]; r_qk = [Res(), Res()]
            vv = [sb(sAtt, "vv%d" % i, [128, 2, 64]) for i in range(2)]; r_vv = [Res(), Res()]
            sq5_l = [sb(sAtt, "sq5%d" % i, [128, 10, 64]) for i in range(2)]; r_sq5_l = [Res(), Res()]
            s10_l = [sb(sAtt, "s10%d" % i, [128, 10]) for i in range(2)]; r_s10_l = [Res(), Res()]
            kn_l = [sb(sAtt, "kn%d" % i, [128, 10, 64]) for i in range(2)]; r_kn_l = [Res(), Res()]
            ra_l = [sb(sAtt, "ra%d" % i, [128, 10, 2, 16]) for i in range(2)]; r_ra_l = [Res() for _ in range(2)]
            rb_l = [sb(sAtt, "rb%d" % i, [128, 10, 2, 16]) for i in range(2)]; r_rb_l = [Res() for _ in range(2)]
            qkr_l = [sb(sAtt, "qkr%d" % i, [128, 10, 64], BF16) for i in range(2)]; r_qkr_l = [Res(), Res()]
            Pt = sb(sAtt, "Pt", [128, 5, 4, 128], BF16); r_Pt = [Res() for _ in range(5)]
            zb = [sb(sAtt, "zb%d" % i, [128, 8, 64]) for i in range(2)]; r_zb = [Res(), Res()]
            den = sb(sAtt, "den", [128, 4]); r_den = Res()
            ob = sb(sAtt, "ob", [128, 4, 64]); r_ob = Res()
            ybt_l = [sb(sAtt, "ybt%d" % i, [128, 8, 64], BF16) for i in range(2)]; r_ybt_l = [Res(), Res()]

            def ld5(t):
                i = t % 2
                p.dma("sp", qk[i][:, 8:10, :].rearrange("p h d -> p (h d)"), tokp_s[t * 128:(t + 1) * 128, TP_KB:TP_KB + 128],
                      reads=[r_tokp], writes=[r_qk[i]])
                p.dma("sp", vv[i][:].rearrange("p h d -> p (h d)"), tokp_s[t * 128:(t + 1) * 128, TP_VB:TP_VB + 128],
                      reads=[r_tokp], writes=[r_vv[i]])
                if t >= t_lo:
                    p.dma("sp", qk[i][:, 0:8, :].rearrange("p h d -> p (h d)"), tokp_s[t * 128:(t + 1) * 128, TP_QB:TP_QB + 512],
                          reads=[r_tokp], writes=[r_qk[i]])
                else:
                    p.op("pool", lambda g, i=i: g.memset(qk[i][:, 0:8, :], 0.0), writes=[r_qk[i]])

            def ldzb(t):
                p.dma("sp", zb[t % 2][:].rearrange("p h d -> p (h d)"), tokp_s[t * 128:(t + 1) * 128, TP_ZB:TP_ZB + 512],
                      reads=[r_tokp], writes=[r_zb[t % 2]])

            def att_prep(t):
                i = t % 2
                Q_ = qk[i]
                sq5, r_sq5, s10, r_s10, kn, r_kn, qkr, r_qkr = sq5_l[i], r_sq5_l[i], s10_l[i], r_s10_l[i], kn_l[i], r_kn_l[i], qkr_l[i], r_qkr_l[i]
                ra, r_ra, rb_, r_rb = ra_l[0], r_ra_l[0], rb_l[0], r_rb_l[0]
                ra2, r_ra2, rb2, r_rb2 = ra_l[1], r_ra_l[1], rb_l[1], r_rb_l[1]
                p.op("act", lambda a: a.copy(V1[:, t, :, 0:64], vv[i][:]), reads=[r_vv[i]], writes=[r_V1[t]])
                p.op("pool", lambda g: g.tensor_tensor(out=sq5[:], in0=Q_[:], in1=Q_[:], op=ALU.mult), reads=[r_qk[i]], writes=[r_sq5])
                yield
                p.op("dve", lambda v: v.reduce_sum(out=s10[:], in_=sq5[:], axis=AX.X), reads=[r_sq5], writes=[r_s10])
                p.op("dve", lambda v: v.tensor_scalar(out=s10[:], in0=s10[:], scalar1=1.0 / 64, scalar2=EPS, op0=ALU.mult, op1=ALU.add),
                     reads=[r_s10], writes=[r_s10])
                yield
                p.op("act", lambda a: a.activation(out=s10[:], in_=s10[:], func=AF.Sqrt), reads=[r_s10], writes=[r_s10])
                yield
                p.op("dve", lambda v: v.reciprocal(s10[:], s10[:]), reads=[r_s10], writes=[r_s10])
                p.op("dve", lambda v: v.tensor_tensor(out=kn[:], in0=Q_[:], in1=s10[:].unsqueeze(2).to_broadcast([128, 10, 64]), op=ALU.mult),
                     reads=[r_qk[i], r_s10], writes=[r_kn])
                yield
                if t < NTC:
                    p.op("dve", lambda v: v.tensor_tensor(out=qkr[:], in0=kn[:], in1=nwqk[:], op=ALU.mult), reads=[r_kn, r_nw2], writes=[r_qkr])
                else:
                    p.op("pool", lambda v: v.tensor_tensor(out=kn[:], in0=kn[:], in1=nwqk[:], op=ALU.mult), reads=[r_kn, r_nw2], writes=[r_kn])
                    yield
                    n = t - NTC
                    k5 = kn[:].rearrange("p h (a f j) -> p h a f j", a=2, f=2)
                    o5 = qkr[:].rearrange("p h (a f j) -> p h a f j", a=2, f=2)
                    cosb = rope_t[:, n, 0, :, :].unsqueeze(1).to_broadcast([128, 10, 2, 16])
                    sinb = rope_t[:, n, 1, :, :].unsqueeze(1).to_broadcast([128, 10, 2, 16])
                    x1 = k5[:, :, :, 0, :]
                    x2 = k5[:, :, :, 1, :]
                    p.op("dve", lambda v: v.tensor_tensor(out=ra[:], in0=x1, in1=cosb, op=ALU.mult), reads=[r_kn, r_const], writes=[r_ra])
                    p.op("pool", lambda g: g.tensor_tensor(out=rb_[:], in0=x2, in1=sinb, op=ALU.mult), reads=[r_kn, r_const], writes=[r_rb])
                    p.op("dve", lambda v: v.tensor_tensor(out=ra2[:], in0=x2, in1=cosb, op=ALU.mult), reads=[r_kn, r_const], writes=[r_ra2])
                    p.op("pool", lambda g: g.tensor_tensor(out=rb2[:], in0=x1, in1=sinb, op=ALU.mult), reads=[r_kn, r_const], writes=[r_rb2])
                    yield
                    p.op("dve", lambda v: v.tensor_tensor(out=o5[:, :, :, 0, :], in0=ra[:], in1=rb_[:], op=ALU.subtract), reads=[r_ra, r_rb], writes=[r_qkr])
                    p.op("pool", lambda v: v.tensor_tensor(out=o5[:, :, :, 1, :], in0=ra2[:], in1=rb2[:], op=ALU.add), reads=[r_ra2, r_rb2], writes=[r_qkr])
                yield
                bk, br = gps()
                bkb = bk[:].bitcast(BF16)

                def tr5q(tt):
                    for h in range(8):
                        ii = tt.transpose(bkb[0:64, h * 128:(h + 1) * 128], qkr[:, h, :], ident_b[:])
                    return ii
                p.op("pe", tr5q, reads=[r_qkr, r_const], writes=[br])
                yield
                p.op("act", lambda a: a.copy(QTt[t % 3][:], bkb[0:64, :].rearrange("p (h n) -> p h n", h=8)), reads=[br], writes=[r_QTt[t % 3]])
                grel(bk)
                bk2, br2 = gps()
                bkb2 = bk2[:].bitcast(BF16)

                def tr5k(tt):
                    for h in range(2):
                        ii = tt.transpose(bkb2[0:64, h * 128:(h + 1) * 128], qkr[:, 8 + h, :], ident_b[:])
                    return ii
                p.op("pe", tr5k, reads=[r_qkr, r_const], writes=[br2])
                yield
                p.op("act", lambda a: a.copy(KT[:, :, t * 128:(t + 1) * 128], bkb2[0:64, 0:256].rearrange("p (h n) -> p h n", h=2)),
                     reads=[br2], writes=[r_KT[t]])
                grel(bk2)
                yield

            def att_main(tq):
                if tq < NTC:
                    keys = [(0, None), (1, None)]
                else:
                    keys = []
                    if tq - 1 >= NTC:
                        keys.append((tq - 1, mk_prev))
                    keys.append((tq, None))
                    if tq + 1 < NT:
                        keys.append((tq + 1, mk_next))
                    keys += [(0, None), (1, None)]
                ybt, r_ybt = ybt_l[tq % 2], r_ybt_l[tq % 2]
                QX, rQX = QTt[tq % 3], r_QTt[tq % 3]
                for hk in range(2):
                    for i, (kt, mk) in enumerate(keys):
                        bS, rS = gps()
                        p.op("pe", lambda tt, bS=bS, kt=kt: tt.matmul(
                            bS[:].rearrange("p (g n) -> p g n", g=4), KT[:, hk, kt * 128:(kt + 1) * 128],
                            QX[:, hk * 4:(hk + 1) * 4, :], start=True, stop=True),
                            reads=[r_KT[kt], rQX], writes=[rS])
                        yield
                        p.op("act", lambda a, bS=bS, i=i: a.activation(out=Pt[:, i, :, :], in_=bS[:].rearrange("p (g n) -> p g n", g=4),
                                                                       func=AF.Exp, scale=0.125), reads=[rS], writes=[r_Pt[i]])
                        grel(bS)
                        if mk is not None:
                            p.op("pool", lambda v, i=i, mk=mk: v.tensor_tensor(out=Pt[:, i, :, :], in0=Pt[:, i, :, :],
                                                                              in1=mk[:].unsqueeze(1).to_broadcast([128, 4, 128]), op=ALU.mult),
                                 reads=[r_Pt[i], r_const], writes=[r_Pt[i]])
                        yield
                    bO, rO = gps()

                    def mmo(tt, bO=bO, hk=hk):
                        for g in range(4):
                            for i, (kt, mk) in enumerate(keys):
                                ii = tt.matmul(bO[:, g * 65:(g + 1) * 65], Pt[:, i, g, :], V1[:, kt, hk, :],
                                               start=(g == 0 and i == 0), stop=(i == len(keys) - 1), skip_group_check=True)
                        return ii
                    p.op("pe", mmo, reads=r_Pt[0:len(keys)] + [r_V1[kt] for kt, _ in keys], writes=[rO])
                    yield
                    bOv = bO[:, 0:260].rearrange("p (g d) -> p g d", g=4)
                    p.op("dve", lambda v, bOv=bOv, hk=hk: v.tensor_tensor(out=den[:], in0=bOv[:, :, 64], in1=esink[:, hk * 4:(hk + 1) * 4], op=ALU.add),
                         reads=[rO, r_esink], writes=[r_den])
                    p.op("dve", lambda v: v.reciprocal(den[:], den[:]), reads=[r_den], writes=[r_den])
                    p.op("dve", lambda v, bOv=bOv: v.tensor_tensor(out=ob[:], in0=bOv[:, :, 0:64], in1=den[:].unsqueeze(2).to_broadcast([128, 4, 64]), op=ALU.mult),
                         reads=[rO, r_den], writes=[r_ob])
                    grel(bO)
                    p.op("pool", lambda v, hk=hk: v.tensor_tensor(out=ybt[:, hk * 4:(hk + 1) * 4, :], in0=ob[:], in1=zb[tq % 2][:, hk * 4:(hk + 1) * 4, :], op=ALU.mult),
                         reads=[r_ob, r_zb[tq % 2]], writes=[r_ybt])
                    yield
                bY, rY = gps()
                bYb = bY[:].bitcast(BF16).rearrange("p (h j) -> p h j", h=8)
                ybt2 = ybt[:].rearrange("p h d -> p (h d)")

                def try_(tt):
                    for c in range(4):
                        ii = tt.transpose(bYb[:, c, :], ybt2[:, c * 128:(c + 1) * 128], ident_b[:])
                    return ii
                p.op("pe", try_, reads=[r_ybt, r_const], writes=[rY])
                yield
                p.op("act", lambda a: a.copy(ybT[:, :, tq * 128:(tq + 1) * 128], bYb[:, 0:4, :]), reads=[rY], writes=[r_ybT[tq]])
                grel(bY)
                yield

            def att_chain():
                ld5(0)
                ldzb(t_lo)
                for t in range(NT):
                    if t + 1 < NT:
                        ld5(t + 1)
                    yield from att_prep(t)
                    tq = t - 1
                    if tq >= t_lo:
                        if tq + 1 < NT:
                            ldzb(tq + 1)
                        yield from att_main(tq)
                if NT - 1 >= t_lo:
                    yield from att_main(NT - 1)

            with ExitStack() as s4o:
                Sst = sb(s4o, "Sst", [128, 2, 4, 128]); r_S = [Res(), Res()]
                s4 = ExitStack()
                Sb_ = sb(s4, "Sb", [128, 2, 4, 128], BF16); r_Sb = [Res(), Res()]
                p.op("pool", lambda g: (g.memset(Sst[:], 0.0), g.memset(Sb_[:], 0.0))[1], writes=r_S + r_Sb)

                class TS:
                    pass

                def mk_ts(d):
                    T_ = TS()
                    for nm, shp, dt in (("dg", [128, 4, 128], F32), ("mg", [128, 4, 128], F32), ("gcb", [128, 4, 128], F32), ("E_", [128, 4, 128], F32),
                                        ("egr", [128, 4, 128], BF16), ("W1", [128, 4, 128], BF16),
                                        ("NQ", [128, 8, 128], BF16), ("NQT", [128, 8, 128], BF16), ("X0", [128, 4, 256], BF16),
                                        ("wb", [128, 4, 128], BF16), ("wTn", [128, 4, 128], BF16), ("vnew", [128, 4, 128], BF16),
                                        ("kd", [128, 4, 128], BF16), ("qdT", [128, 4, 128], BF16), ("M1s", [128, 2, 4, 128], BF16)):
                        setattr(T_, nm, sb(s4, "%s_d%d" % (nm, d), shp, dt))
                        setattr(T_, "r_" + nm, Res())
                    T_.TT = sb(s4, "TT_d%d" % d, [128, 2, 4, 128], BF16)
                    T_.r_TT = Res()
                    T_.kv = [sb(s4, "kv%d_d%d" % (i, d), [128, 2, 4, 128], BF16) for i in range(2)]
                    T_.r_kv = [Res(), Res()]
                    T_.ost = [sb(s4, "ost%d_d%d" % (i, d), [128, 4, 128]) for i in range(2)]
                    T_.r_ost = [Res(), Res()]
                    T_.nstep = 0
                    return T_
                TSS = [mk_ts(0), mk_ts(1)]

                def bc_h(ap2):
                    return ap2.unsqueeze(1).to_broadcast([128, 4, 128])

                def bc_j(ap2):
                    return ap2.unsqueeze(2).to_broadcast([128, 4, 128])

                def v4(bank):
                    return bank[:].rearrange("p (h j) -> p h j", h=4)

                def ld_kv(d, t, slot):
                    T_ = TSS[d]
                    p.dma("sp", T_.kv[slot][:, 0], ktok_s[t], reads=[r_ktok_s], writes=[T_.r_kv[slot]])
                    p.dma("sp", T_.kv[slot][:, 1], vtok_s[t], reads=[r_vtok_s], writes=[T_.r_kv[slot]])

                def gdn_step(d, t, with_out, t_next):
                    T_ = TSS[d]
                    slot = T_.nstep % 2
                    T_.nstep += 1
                    if t_next is not None:
                        ld_kv(d, t_next, 1 - slot)
                    ktk = T_.kv[slot][:, 0]
                    vtk = T_.kv[slot][:, 1]
                    r_kvs = [T_.r_kv[slot]]
                    dg, mg, E_, egr, W1, NQ, NQT, X0 = T_.dg, T_.mg, T_.E_, T_.egr, T_.W1, T_.NQ, T_.NQT, T_.X0
                    tA = T_.gcb
                    T_.r_tA = T_.r_gcb
                    wb, wTn, vnew, kd, qdT, M1s = T_.wb, T_.wTn, T_.vnew, T_.kd, T_.qdT, T_.M1s
                    dh = slice(d * 4, d * 4 + 4)
                    ck = slice(t * 128, (t + 1) * 128)
                    rd_qk = r_qT + r_kT
                    p.op("dve", lambda g: g.tensor_tensor(out=dg[:], in0=bc_h(ident_f[:]), in1=bc_j(gcs[:, t, dh]), op=ALU.mult),
                         reads=[r_const, r_gate], writes=[T_.r_dg])
                    p.op("pool", lambda g: g.tensor_tensor(out=mg[:], in0=bc_h(m_incl[:, d, :]), in1=bc_j(gcs[:, t, dh]), op=ALU.subtract),
                         reads=[r_const, r_gate], writes=[T_.r_mg])
                    p.op("pool", lambda g: g.tensor_tensor(out=W1[:], in0=bc_h(m_strict[:, d, :]), in1=bc_j(beta[:, t, dh]), op=ALU.mult),
                         reads=[r_const, r_gate], writes=[T_.r_W1])
                    def late_pool(which):
                        if which == 0:
                            p.op("pool", lambda g: g.tensor_tensor(out=X0[:, :, 0:128], in0=vtk, in1=bc_j(beta[:, t, dh]), op=ALU.mult),
                                 reads=r_kvs + [r_gate], writes=[T_.r_X0])
                        elif which == 1:
                            p.op("pool", lambda g: g.tensor_tensor(out=X0[:, :, 128:256], in0=ktk, in1=bc_j(bge[:, t, dh]), op=ALU.mult),
                                 reads=r_kvs + [r_gate], writes=[T_.r_X0])
                        else:
                            p.op("pool", lambda g: g.tensor_tensor(out=kd[:], in0=ktk, in1=bc_j(ekd[:, t, dh]), op=ALU.mult),
                                 reads=r_kvs + [r_gate], writes=[T_.r_kd])
                    yield
                    bB, rB = gps()

                    def mm1(tt):
                        return tt.matmul(bB[:], ones_f[:], dg[:].rearrange("p h j -> p (h j)"), start=True, stop=True)
                    p.op("pe", mm1, reads=[r_const, T_.r_dg], writes=[rB])
                    yield
                    p.op("dve", lambda v: v.tensor_tensor(out=T_.gcb[:], in0=v4(bB), in1=mg[:], op=ALU.add), reads=[rB, T_.r_mg], writes=[T_.r_gcb])
                    if with_out:
                        p.op("act", lambda a: a.activation(out=egr[:], in_=v4(bB), func=AF.Exp), reads=[rB, T_.r_gcb], writes=[T_.r_egr])
                    p.op("act", lambda a: a.activation(out=E_[:], in_=T_.gcb[:], func=AF.Exp, scale=-1.0), reads=[T_.r_gcb], writes=[T_.r_E_])
                    grel(bB)
                    pK, rKQ, iK = gps2()
                    pKv = pK.rearrange("p w (h j) -> p w h j", h=4)

                    def mm2(tt):
                        for h in range(4):
                            tt.matmul(pKv[:, 0, h, :], kT[:, h, ck], kT[:, h, ck], start=True, stop=True)
                        for h in range(4):
                            i = tt.matmul(pKv[:, 1, h, :], qT[:, h, ck], kT[:, h, ck], start=True, stop=True)
                        return i
                    p.op("pe", mm2, reads=rd_qk, writes=list(rKQ))
                    p.op("pool", lambda g: g.tensor_copy(T_.TT[:].rearrange("p w h j -> p (w h) j"), ident_b[:].unsqueeze(1).to_broadcast([128, 8, 128])),
                         reads=[r_const], writes=[T_.r_TT])
                    yield
                    p.op("dve", lambda v: v.tensor_tensor(out=NQ[:].rearrange("p (w h) j -> p w h j", w=2), in0=pKv,
                                                          in1=E_[:].unsqueeze(1).to_broadcast([128, 2, 4, 128]), op=ALU.mult),
                         reads=list(rKQ) + [T_.r_E_], writes=[T_.r_NQ])
                    p.op("dve", lambda v: v.tensor_tensor(out=NQ[:, 0:4, :], in0=NQ[:, 0:4, :], in1=W1[:], op=ALU.mult), reads=[T_.r_NQ, T_.r_W1], writes=[T_.r_NQ])
                    grel2(iK)
                    yield
                    bT, rT = gps()
                    bTb = bT[:].bitcast(BF16).rearrange("p (h j) -> p h j", h=8)

                    def tr1(tt):
                        for i in range(8):
                            ii = tt.transpose(bTb[:, i, :], NQ[:, i, :], ident_b[:])
                        return ii
                    p.op("pe", tr1, reads=[T_.r_NQ, r_const], writes=[rT])
                    yield
                    p.op("act", lambda a: a.copy(NQT[:], bTb), reads=[rT], writes=[T_.r_NQT])
                    grel(bT)

                    yield
                    TT, rTT_ = T_.TT, T_.r_TT
                    um2 = gmask[:, d].bitcast(mybir.dt.uint16)

                    def m2(li):
                        return um2[:, li, :, :].unsqueeze(2).to_broadcast([128, 2, 4, 128])
                    p.op("dve", lambda v: v.copy_predicated(TT[:, 0], bc_h(um2[:, 0, 0, :]), NQ[:, 0:4, :]), reads=[T_.r_NQ, r_const, rTT_], writes=[rTT_])
                    p.op("dve", lambda v: v.copy_predicated(TT[:, 1], bc_h(um2[:, 0, 1, :]), NQT[:, 0:4, :]), reads=[T_.r_NQT, r_const, rTT_], writes=[rTT_])
                    yield
                    for li in range(1, 7):
                        if 2 <= li <= 4:
                            late_pool(li - 2)
                        elif li == 5 and with_out:
                            p.op("pool", lambda g: g.tensor_tensor(out=qdT[:], in0=qT[:, :, ck], in1=egr[:], op=ALU.mult),
                                 reads=r_qT + [T_.r_egr], writes=[T_.r_qdT])
                        pA, rA, iA = gps2()
                        pAv = pA.rearrange("p w (h j) -> p w h j", h=4)

                        def mmA(tt, pAv=pAv):
                            for h in range(4):
                                tt.matmul(pAv[:, 0, h, :], NQT[:, h, :], TT[:, 0, h, :], start=True, stop=True)
                            for h in range(4):
                                i = tt.matmul(pAv[:, 1, h, :], NQ[:, h, :], TT[:, 1, h, :], start=True, stop=True)
                            return i
                        p.op("pe", mmA, reads=[T_.r_NQ, T_.r_NQT, rTT_], writes=list(rA))
                        yield
                        p.op("act", lambda a, pAv=pAv: a.copy(M1s[:], pAv), reads=list(rA), writes=[T_.r_M1s])
                        grel2(iA)
                        yield
                        pB, rB2_, iB = gps2()
                        pBv = pB.rearrange("p w (h j) -> p w h j", h=4)

                        def mmB(tt, pBv=pBv):
                            for h in range(4):
                                tt.matmul(pBv[:, 0, h, :], TT[:, 1, h, :], M1s[:, 0, h, :], start=True, stop=True)
                            for h in range(4):
                                i = tt.matmul(pBv[:, 1, h, :], TT[:, 0, h, :], M1s[:, 1, h, :], start=True, stop=True)
                            return i
                        p.op("pe", mmB, reads=[T_.r_M1s, rTT_], writes=list(rB2_))
                        yield
                        p.op("dve", lambda v, pBv=pBv, li=li: v.copy_predicated(TT[:], m2(li), pBv), reads=list(rB2_) + [r_const], writes=[rTT_])
                        grel2(iB)
                        yield
                    pX, rX, iX = gps2()
                    pXv = pX.rearrange("p w (h j) -> p w h j", h=2)

                    def xv(h):
                        return pX[:, h // 2, (h % 2) * 256:(h % 2) * 256 + 256]

                    def mm3(tt):
                        for h in range(4):
                            i = tt.matmul(xv(h), TT[:, 1, h, :], X0[:, h, :], start=(h % 2 == 0), stop=True, skip_group_check=True)
                        return i
                    p.op("pe", mm3, reads=[T_.r_X0, rTT_], writes=list(rX))
                    yield
                    p.op("act", lambda a: a.copy(wb[:].rearrange("p (w h) j -> p w h j", w=2), pXv[:, :, :, 128:256]),
                         reads=list(rX), writes=[T_.r_wb])
                    yield
                    bW, rW = gps()
                    bWb = bW[:].bitcast(BF16).rearrange("p (h j) -> p h j", h=8)

                    def tr2(tt):
                        for h in range(4):
                            ii = tt.transpose(bWb[:, h, :], wb[:, h, :], ident_b[:])
                        return ii
                    p.op("pe", tr2, reads=[T_.r_wb, r_const], writes=[rW])
                    yield
                    p.op("dve", lambda v: v.tensor_scalar(out=wTn[:], in0=bWb[:, 0:4, :], scalar1=-1.0, scalar2=None, op0=ALU.mult),
                         reads=[rW], writes=[T_.r_wTn])
                    grel(bW)
                    yield

                    def mm6(tt):
                        for h in range(4):
                            i = tt.matmul(xv(h)[:, 0:128], wTn[:, h, :], Sb_[:, d, h, :], start=False, stop=True, skip_group_check=True)
                        return i
                    p.op("pe", mm6, reads=[T_.r_wTn, r_Sb[d]], writes=list(rX))
                    yield
                    p.op("act", lambda a: a.copy(vnew[:].rearrange("p (w h) j -> p w h j", w=2), pXv[:, :, :, 0:128]),
                         reads=list(rX), writes=[T_.r_vnew])
                    grel2(iX)
                    yield
                    if with_out:
                        bO, rO = gps()

                        def mm7(tt):
                            for h in range(4):
                                tt.matmul(v4(bO)[:, h, :], qdT[:, h, :], Sb_[:, d, h, :], start=True, stop=False)
                                i = tt.matmul(v4(bO)[:, h, :], NQT[:, 4 + h, :], vnew[:, h, :], start=False, stop=True)
                            return i
                        p.op("pe", mm7, reads=[T_.r_qdT, r_Sb[d], T_.r_NQT, T_.r_vnew], writes=[rO])
                    bS, rS = gps()

                    def mm8(tt):
                        for h in range(4):
                            i = tt.matmul(v4(bS)[:, h, :], kd[:, h, :], vnew[:, h, :], start=True, stop=True)
                        return i
                    p.op("pe", mm8, reads=[T_.r_kd, T_.r_vnew], writes=[rS])
                    p.op("pool", lambda g: g.tensor_tensor(out=Sst[:, d, :, :], in0=Sst[:, d, :, :], in1=bc_j(egl[:, t, dh]), op=ALU.mult),
                         reads=[r_S[d], r_gate], writes=[r_S[d]])
                    yield
                    p.op("dve", lambda v: v.tensor_tensor(out=Sst[:, d, :, :], in0=Sst[:, d, :, :], in1=v4(bS), op=ALU.add),
                         reads=[r_S[d], rS], writes=[r_S[d]])
                    p.op("act", lambda a: a.copy(Sb_[:, d, :, :], Sst[:, d, :, :]), reads=[r_S[d]], writes=[r_Sb[d]])
                    if with_out:
                        OS, rOS = T_.ost[slot], T_.r_ost[slot]
                        p.op("act", lambda a: a.copy(OS[:], v4(bO)), reads=[rO], writes=[rOS])
                        p.dma("act", o_s[d, t * 128:(t + 1) * 128, :], OS[:].rearrange("p h j -> p (h j)"), reads=[rOS], writes=[r_o_s[d][t]], key=rOS)
                        grel(bO)
                    grel(bS)
                    yield

                def chain(d, order):
                    ld_kv(d, order[0], 0)
                    for i_, t in enumerate(order):
                        yield from gdn_step(d, t, upd_ctx or t >= NTC, order[i_ + 1] if i_ + 1 < len(order) else None)

                gens = [chain(0, list(range(0, NT))), chain(1, [1, 0] + list(range(NT - 1, NTC - 1, -1))), att_chain()]
                alive = [True, True, True]
                rnd = 0
                for _ in range(GDN_PHASE):
                    next(gens[0])
                while any(alive):
                    rnd += 1
                    for gi_ in (0, 2, 1):
                        if alive[gi_]:
                            try:
                                next(gens[gi_])
                            except StopIteration:
                                alive[gi_] = False
                nrot[0] = 8
                dbg_dump("Sfin", Sst[:], r_S, [128, 2, 4, 128], direct=True)
                p.barrier()
                s4.close()
            sAtt.close()

        stY = ExitStack()
        yaT = sb(stY, "yaT", [128, 4, LT], BF16); r_yaT = [Res() for _ in range(NT)]

        def bc_h(ap2):
            return ap2.unsqueeze(1).to_broadcast([128, 4, 128])

        def bc_j(ap2):
            return ap2.unsqueeze(2).to_broadcast([128, 4, 128])
        s6 = ExitStack()
        if True:
            gnw = sb(s6, "gnw", [128, 128]); r_gnw = Res()
            D6 = 3
            of = [sb(s6, "of%d" % i, [128, 4, 128]) for i in range(D6)]; r_of = [Res() for _ in range(D6)]
            ob6 = [sb(s6, "ob6%d" % i, [128, 4, 128]) for i in range(D6)]; r_ob6 = [Res() for _ in range(D6)]
            za = [sb(s6, "za%d" % i, [128, 4, 128]) for i in range(D6)]; r_za = [Res() for _ in range(D6)]
            sq6_l = [sb(s6, "sq6%d" % i, [128, 4, 128]) for i in range(D6)]; r_sq6_l = [Res() for _ in range(D6)]
            s4t_l = [sb(s6, "s4t%d" % i, [128, 4]) for i in range(D6)]; r_s4_l = [Res() for _ in range(D6)]
            t1_l = [sb(s6, "t1%d" % i, [128, 4, 128]) for i in range(D6)]; r_t1_l = [Res() for _ in range(D6)]
            t2_l = [sb(s6, "t2%d" % i, [128, 4, 128]) for i in range(D6)]; r_t2_l = [Res() for _ in range(D6)]
            yab_l = [sb(s6, "yab%d" % i, [128, 4, 128], BF16) for i in range(D6)]; r_yab_l = [Res() for _ in range(D6)]
            p.dma("sp", gnw[:], gdn_norm_w[l:l + 1, :].to_broadcast([128, 128]), writes=[r_gnw])

            def p6_tile(t):
                i = (t - t_lo) % D6
                sq6, r_sq6, s4t, r_s4, t1, r_t1, t2, r_t2, yab, r_yab = (sq6_l[i], r_sq6_l[i], s4t_l[i], r_s4_l[i], t1_l[i], r_t1_l[i],
                                                                          t2_l[i], r_t2_l[i], yab_l[i], r_yab_l[i])
                p.dma("sp", za[i][:].rearrange("p h j -> p (h j)"), tokp_s[t * 128:(t + 1) * 128, TP_ZA:TP_ZA + 512],
                      reads=[r_tokp], writes=[r_za[i]])
                p.dma("sp", of[i][:].rearrange("p h j -> p (h j)"), o_s[0, t * 128:(t + 1) * 128, :], reads=[r_o_s[0][t]], writes=[r_of[i]])
                p.dma("sp", ob6[i][:].rearrange("p h j -> p (h j)"), o_s[1, t * 128:(t + 1) * 128, :], reads=[r_o_s[1][t]], writes=[r_ob6[i]])
                yield
                O = of[i][:]
                p.op("pool", lambda g: g.tensor_tensor(out=O, in0=O, in1=ob6[i][:], op=ALU.add), reads=[r_of[i], r_ob6[i]], writes=[r_of[i]])
                p.op("pool", lambda g: g.tensor_tensor(out=sq6[:], in0=O, in1=O, op=ALU.mult), reads=[r_of[i]], writes=[r_sq6])
                p.op("pool", lambda g: g.tensor_tensor(out=t2[:], in0=za[i][:], in1=bc_h(gnw[:]), op=ALU.mult), reads=[r_za[i], r_gnw], writes=[r_t2])
                yield
                p.op("dve", lambda v: v.reduce_sum(out=s4t[:], in_=sq6[:], axis=AX.X), reads=[r_sq6], writes=[r_s4])
                p.op("dve", lambda v: v.tensor_scalar(out=s4t[:], in0=s4t[:], scalar1=1.0 / 128, scalar2=EPS, op0=ALU.mult, op1=ALU.add),
                     reads=[r_s4], writes=[r_s4])
                yield
                p.op("act", lambda a: a.activation(out=s4t[:], in_=s4t[:], func=AF.Sqrt), reads=[r_s4], writes=[r_s4])
                yield
                p.op("dve", lambda v: v.reciprocal(s4t[:], s4t[:]), reads=[r_s4], writes=[r_s4])
                p.op("dve", lambda v: v.tensor_tensor(out=t1[:], in0=O, in1=bc_j(s4t[:]), op=ALU.mult), reads=[r_of[i], r_s4], writes=[r_t1])
                p.op("dve", lambda v: v.tensor_tensor(out=yab[:], in0=t1[:], in1=t2[:], op=ALU.mult), reads=[r_t1, r_t2], writes=[r_yab])
                yield
                bk, br = gps()
                bkb = bk[:].bitcast(BF16).rearrange("p (h j) -> p h j", h=8)

                def tr6(tt):
                    for h in range(4):
                        ii = tt.transpose(bkb[:, h, :], yab[:, h, :], ident_b[:])
                    return ii
                p.op("pe", tr6, reads=[r_yab, r_const], writes=[br])
                yield
                p.op("act", lambda a: a.copy(yaT[:, :, t * 128:(t + 1) * 128], bkb[:, 0:4, :]), reads=[br], writes=[r_yaT[t]])
                grel(bk)
                yield
        dbg_dump("ybT", ybT[:], r_ybT[t_lo:], [128, 4, LT])

        with ExitStack() as s7:
            wpa = sb(s7, "wpa", [128, 4, D], BF16)
            wpb = sb(s7, "wpb", [128, 4, D], BF16)
            wo = sb(s7, "wo", [128, 8, D], BF16)
            r_w7 = Res()
            G_bc = [sb(s7, "G_bc%d" % i, [128, D]) for i in range(2)]
            r_bc = Res()
            for i, r in enumerate((b, NB)):
                p.dma("sp", G_bc[i][:], mods_s[r:r + 1, 2 * D:3 * D].to_broadcast([128, D]), reads=[r_mods], writes=[r_bc])
            p.dma("pool", wpa[:], w_proj_a[l].rearrange("(k p) n -> p k n", p=128), writes=[r_w7])
            p.dma("pool", wpb[:], w_proj_b[l].rearrange("(k p) n -> p k n", p=128), writes=[r_w7])
            p.dma("pool", wo[:], w_out[l].rearrange("(k p) n -> p k n", p=128), writes=[r_w7])
            NGB = 4
            gab = [sb(s7, "gab%d" % i, [128, 2, 512], BF16) for i in range(NGB)]; r_gab = [Res() for _ in range(NGB)]
            y1_l = [sb(s7, "y1%d" % i, [128, 512]) for i in range(3)]; r_y1_l = [Res() for _ in range(3)]
            y2_l = [sb(s7, "y2%d" % i, [128, 512]) for i in range(3)]; r_y2_l = [Res() for _ in range(3)]
            yT_l = [sb(s7, "yT%d" % i, [128, 8, 512], BF16) for i in range(3)]; r_yT_l = [Res() for _ in range(3)]
            xr = [sb(s7, "xr%d" % i, [128, D]) for i in range(3)]; r_xr = [Res() for _ in range(3)]
            xo = [sb(s7, "xo%d" % i, [128, D]) for i in range(3)]; r_xo = [Res() for _ in range(3)]
            tok0 = t_lo * 128
            nx = 0
            groups7 = [(g0, min(512, LT - g0)) for g0 in range(tok0, LT, 512)]

            def m_fc(gi7, fc):
                g0, gw = groups7[gi7]
                yT, r_yT = yT_l[gi7 % 3], r_yT_l[gi7 % 3]
                idx = gi7 * 8 + fc
                G_ = gab[idx % NGB]; rG = r_gab[idx % NGB]
                y1, r_y1, y2, r_y2 = y1_l[idx % 3], r_y1_l[idx % 3], y2_l[idx % 3], r_y2_l[idx % 3]
                p.dma("sp", G_[:, 0, 0:gw], gT_s[fc * 128:(fc + 1) * 128, g0:g0 + gw], reads=[r_gT], writes=[rG])
                p.dma("sp", G_[:, 1, 0:gw], gT_s[1024 + fc * 128:1024 + (fc + 1) * 128, g0:g0 + gw], reads=[r_gT], writes=[rG])
                yield
                bA, rA = gps()
                bB_, rB_ = gps()

                def mm7a(tt):
                    for k in range(4):
                        tt.matmul(bA[:, 0:gw], wpa[:, k, fc * 128:(fc + 1) * 128], yaT[:, k, g0:g0 + gw], start=(k == 0), stop=(k == 3))
                    for k in range(4):
                        i = tt.matmul(bB_[:, 0:gw], wpb[:, k, fc * 128:(fc + 1) * 128], ybT[:, k, g0:g0 + gw], start=(k == 0), stop=(k == 3))
                    return i
                p.op("pe", mm7a, reads=[r_w7] + r_yaT[g0 // 128:(g0 + gw) // 128] + r_ybT[g0 // 128:(g0 + gw) // 128], writes=[rA, rB_])
                yield
                p.op("dve", lambda v: v.tensor_tensor(out=y1[:, 0:gw], in0=bA[:, 0:gw], in1=G_[:, 0, 0:gw], op=ALU.mult), reads=[rA, rG], writes=[r_y1])
                p.op("dve", lambda v: v.tensor_tensor(out=y2[:, 0:gw], in0=bB_[:, 0:gw], in1=G_[:, 1, 0:gw], op=ALU.mult), reads=[rB_, rG], writes=[r_y2])
                grel(bA, bB_)
                yield
                p.op("pool", lambda g: g.tensor_tensor(out=yT[:, fc, 0:gw], in0=y1[:, 0:gw], in1=y2[:, 0:gw], op=ALU.add),
                     reads=[r_y1, r_y2], writes=[r_yT])
                yield

            ntile7 = [0]

            def m_tile(gi7, tt_):
                g0, gw = groups7[gi7]
                yT, r_yT = yT_l[gi7 % 3], r_yT_l[gi7 % 3]
                t = g0 // 128 + tt_
                j = ntile7[0] % 3
                ntile7[0] += 1
                XR, rXR = xr[j], r_xr[j]
                XO, rXO = xo[j], r_xo[j]
                p.dma("sp", XR[:], tok_src(t), reads=src_res, writes=[rXR])
                gi = 1 if t < NTC else 0
                yield
                for half in range(2):
                    bk, br = gps()

                    def mm7b(tt, bk=bk, half=half):
                        for k in range(8):
                            i = tt.matmul(bk[:], yT[:, k, tt_ * 128:(tt_ + 1) * 128], wo[:, k, half * 512:(half + 1) * 512], start=(k == 0), stop=(k == 7))
                        return i
                    p.op("pe", mm7b, reads=[r_yT, r_w7], writes=[br])
                    yield
                    p.op("dve", lambda v, bk=bk, half=half: v.tensor_tensor(out=XO[:, half * 512:(half + 1) * 512], in0=bk[:],
                                                                          in1=G_bc[gi][:, half * 512:(half + 1) * 512], op=ALU.mult),
                         reads=[br, r_bc], writes=[rXO])
                    grel(bk)
                yield
                p.op("pool", lambda g: g.tensor_tensor(out=XO[:], in0=XO[:], in1=XR[:], op=ALU.add), reads=[rXO, rXR], writes=[rXO])
                yield
                if t < NTC:
                    p.dma("sp", ctx1_s[b, t * 128:(t + 1) * 128, :], XO[:], reads=[rXO], writes=[r_ctx1[b]])
                else:
                    p.dma("sp", x_dst[(t - NTC) * 128:(t - NTC + 1) * 128, :], XO[:], reads=[rXO], writes=dst_res, key=rXO)
                yield
            def p6_of(gi7):
                g0, gw = groups7[gi7]
                return [p6_tile(t) for t in range(g0 // 128, (g0 + gw) // 128)]
            ng7 = len(groups7)
            glist = p6_of(0) + [None]
            for gi7 in range(ng7 + 1):
                if gi7 < ng7:
                    glist += [m_fc(gi7, fc) for fc in range(8)]
                if gi7 >= 1:
                    glist += [m_tile(gi7 - 1, tt_) for tt_ in range(groups7[gi7 - 1][1] // 128)]
                if gi7 + 1 < ng7:
                    glist += p6_of(gi7 + 1)
                glist += [None]
            run_pipelined(glist, 3, stagger=1)
            p.barrier()
        dbg_dump("yaT", yaT[:], r_yaT[t_lo:], [128, 4, LT])
        s6.close()
        if dbg and (b, l) == dbg_bl:
            for nm, src, rr in (("x1", x1_s[b], r_x1[b]), ("ctx1", ctx1_s[b], r_ctx1[b])):
                if nm in dbg:
                    p.dma("sp", dbg_out[nm], src, reads=[rr])
            p.barrier()
        stY.close()
        stYb.close()

    dbg_bl = (0, 0)
    for l in range(DEPTH):
        layer_mods(l)
        for b in range(NB):
            block(b, l)
            if dbg:
                break
        if dbg:
            break
    p.barrier()
    top.close()
    return nc, p


def _rope_tables():
    inv = 10000.0 ** (-np.arange(0, 32, 2, dtype=np.float32) / 32.0)
    t = np.arange(L)
    pr = (t // 64).astype(np.float32)
    pc = (t % 64).astype(np.float32)
    ar = pr[:, None] * inv[None, :]
    ac = pc[:, None] * inv[None, :]
    tab = np.stack([np.cos(ar), np.cos(ac), np.sin(ar), np.sin(ac)], axis=1).astype(np.float32)
    return np.ascontiguousarray(tab.reshape(16, 128, 2, 2, 16).transpose(1, 0, 2, 3, 4))


def _gmask():
    base = np.zeros((2, 7, 128, 128), np.float32)
    pp = np.arange(128)[:, None]
    ff = np.arange(128)[None, :]
    for li in range(7):
        s_ = 1 << li
        base[0, li] = (((ff // s_) % 2 == 1) & ((pp // s_) == (ff // s_) - 1)).astype(np.float32)
        base[1, li] = (((pp // s_) % 2 == 1) & ((ff // s_) == (pp // s_) - 1)).astype(np.float32)
    m = np.zeros((128, 2, 7, 2, 128), np.float32)
    for d in range(2):
        for li in range(7):
            m[:, d, li, 0, :] = base[1 - d, li]
            m[:, d, li, 1, :] = base[d, li]
    return m


def make_in_maps(inputs, NB, ncores):
    maps = []
    rope = _rope_tables()
    gm = _gmask()
    for i in range(ncores):
        sl = slice(i * NB, (i + 1) * NB)
        m = {
            "x": np.ascontiguousarray(inputs["x"][sl]),
            "ctx": np.ascontiguousarray(inputs["ctx"][sl]),
            "c": np.ascontiguousarray(np.concatenate([inputs["c"][sl], inputs["c_ctx"][None, :]], axis=0)),
            "rope": rope,
            "gmask": gm,
        }
        for k in ("norm_w", "w_mod", "b_mod", "w_in", "conv_w", "gdn_norm_w", "q_norm_w", "k_norm_w", "sink",
                  "w_proj_a", "w_proj_b", "w_out"):
            m[k] = np.ascontiguousarray(inputs[k])
        m["a_log"] = np.ascontiguousarray(inputs["a_log"].reshape(DEPTH, 8))
        m["dt_bias"] = np.ascontiguousarray(inputs["dt_bias"].reshape(DEPTH, 8))
        maps.append(m)
    return maps


def kernel(**inputs):
    inputs = {k: np.asarray(v, dtype=np.float32) for k, v in inputs.items()}
    NB = 16 // NCORES
    nc, _ = build(NB)
    maps = make_in_maps(inputs, NB, NCORES)
    res = run_bass_kernel_spmd(nc, maps, core_ids=list(range(NCORES)))
    return np.concatenate([r["y"] for r in res.results], axis=0).astype(np.float32)
```
